# Optimizing a Trainium2 kernel written in Bass

```python
import jax, jax.numpy as jnp
from jax import lax
import numpy as np

D_MODEL = 1024
BATCH = 8
SEQ = 2048
DEPTH = 1
DEC_BATCH = 128
DEC_SEQ = 1
PAST_LEN = 16384
PAGE_SIZE = 128

N_META = 16
POOL_WIDTH = D_MODEL // 2
POOL_WINDOWS = (2, 4, 8, 16)
POOL_GROUPS = len(POOL_WINDOWS)
POOL_GROUP = POOL_WIDTH // POOL_GROUPS
POOL_MAXW = max(POOL_WINDOWS)
POOL_BUF = POOL_MAXW - 1
HG_HEADS = 4
HG_WIDTH = D_MODEL // 2
HG_DK = 128
HG_DV = HG_WIDTH // HG_HEADS
HG_KTOT = HG_HEADS * HG_DK
HG_CHUNK = 64
D_FF = 4 * D_MODEL
EPS = 1e-6
SPLITS = (POOL_WIDTH, HG_KTOT, HG_KTOT, HG_WIDTH, HG_WIDTH, D_MODEL, D_MODEL)
N_IN = sum(SPLITS)

kernel_name = 'hybrid_pool_hgrn2_meta_decoder_step'


def _rmsnorm(x, g):
    xf = x.astype(jnp.float32)
    y = xf * lax.rsqrt(jnp.mean(xf * xf, axis=-1, keepdims=True) + EPS) * g.astype(jnp.float32)
    return y.astype(x.dtype)


def _pool_mix(u, pos):
    L = u.shape[1]
    cs = jnp.pad(jnp.cumsum(u, axis=1), ((0, 0), (POOL_MAXW, 0), (0, 0)))
    outs = []
    for gi, w in enumerate(POOL_WINDOWS):
        sl = slice(gi * POOL_GROUP, (gi + 1) * POOL_GROUP)
        wsum = cs[:, POOL_MAXW:, sl] - cs[:, POOL_MAXW - w:POOL_MAXW - w + L, sl]
        cnt = jnp.minimum(pos + 1, w).astype(jnp.float32)
        outs.append(wsum / cnt[None, :, None])
    return jnp.concatenate(outs, axis=-1) - u


def _hgrn_chunk(S0, q, k, v, g):
    C = q.shape[1]
    G = jnp.cumsum(g, axis=1)
    causal = jnp.tril(jnp.ones((C, C), dtype=bool))[None, :, :, None, None]
    diff = G[:, :, None] - G[:, None, :]
    decay = jnp.exp(jnp.where(causal, diff, -jnp.inf))
    A = jnp.einsum('bthk,btshk,bshk->bhts', q, decay, k)
    o = jnp.einsum('bhts,bshv->bthv', A, v) + jnp.einsum('bthk,bhkv->bthv', q * jnp.exp(G), S0)
    G_last = G[:, -1]
    S = jnp.exp(G_last)[..., None] * S0 + jnp.einsum('bshk,bshv->bhkv', k * jnp.exp(G_last[:, None] - G), v)
    return S, o


def _hgrn_scan(S0, q, k, v, g):
    B, L = q.shape[:2]
    n = L // HG_CHUNK

    def blk(a):
        return a.reshape(B, n, HG_CHUNK, *a.shape[2:]).swapaxes(0, 1)

    def step(S, inp):
        return _hgrn_chunk(S, *inp)

    S, o = lax.scan(step, S0, (blk(q), blk(k), blk(v), blk(g)))
    return S, o.swapaxes(0, 1).reshape(B, L, *o.shape[3:])


def _layer(x, pool_prev, S0, start_pos, prompt, lb, g_mix, w_in, w_pool, pool_scale,
           g_onorm, w_a, w_b, w_out, g_mlp, w_up, w_down):
    B, L, _ = x.shape
    f32 = jnp.float32
    h = _rmsnorm(x, g_mix)
    z = h @ w_in
    offs = [int(o) for o in np.cumsum(SPLITS)[:-1]]
    u, q, f, v, og, ga, gb = jnp.split(z, offs, axis=-1)
    u32 = u.astype(f32)
    if pool_prev is None:
        u_cat = u32
        pos = start_pos + jnp.arange(L)
    else:
        u_cat = jnp.concatenate([pool_prev.astype(f32), u32], axis=1)
        pos = start_pos - POOL_BUF + jnp.arange(POOL_BUF + L)
    pooled = _pool_mix(u_cat, pos)[:, -L:].reshape(B, L, POOL_GROUPS, POOL_GROUP)
    new_pool = u_cat[:, -POOL_BUF:].astype(x.dtype)
    ya = jnp.einsum('blgc,gcd->blgd', pooled, w_pool.astype(f32)).reshape(B, L, POOL_WIDTH)
    ya = (ya * pool_scale.astype(f32)).astype(x.dtype) @ w_a
    qh = jax.nn.silu(q.astype(f32)).reshape(B, L, HG_HEADS, HG_DK)
    fg = lb + (1.0 - lb) * jax.nn.sigmoid(f.astype(f32))
    kh = (1.0 - fg).reshape(B, L, HG_HEADS, HG_DK)
    gh = jnp.log(fg).reshape(B, L, HG_HEADS, HG_DK)
    vh = v.astype(f32).reshape(B, L, HG_HEADS, HG_DV)
    S0 = S0.astype(f32)
    if prompt:
        S, o_meta = _hgrn_chunk(S0, qh[:, :N_META], kh[:, :N_META], vh[:, :N_META], gh[:, :N_META])
        S, o_real = _hgrn_scan(S, qh[:, N_META:], kh[:, N_META:], vh[:, N_META:], gh[:, N_META:])
        o = jnp.concatenate([o_meta, o_real], axis=1)
    else:
        S, o = _hgrn_chunk(S0, qh, kh, vh, gh)
    o = _rmsnorm(o, g_onorm) * jax.nn.silu(og.astype(f32)).reshape(B, L, HG_HEADS, HG_DV)
    yb = o.reshape(B, L, HG_WIDTH).astype(x.dtype) @ w_b
    m = jax.nn.sigmoid(ga) * ya + jax.nn.sigmoid(gb) * yb
    x = x + m @ w_out
    h2 = _rmsnorm(x, g_mlp)
    x = x + jnp.square(jax.nn.relu(h2 @ w_up)) @ w_down
    return x, new_pool, S.astype(x.dtype)


def setup_inputs(seed: int = 0) -> dict:
    key = jax.random.key(seed)
    ks = jax.random.split(key, 20)
    nrm = jax.random.normal
    f32 = jnp.float32
    return {
        'x_prompt': nrm(ks[0], (BATCH, SEQ, D_MODEL), f32),
        'x_sample': nrm(ks[1], (DEC_BATCH, DEC_SEQ, D_MODEL), f32),
        'state_pool': nrm(ks[2], (DEPTH, DEC_BATCH, POOL_BUF, POOL_WIDTH), f32) * 0.5,
        'state_hgrn': nrm(ks[3], (DEPTH, DEC_BATCH, HG_HEADS, HG_DK, HG_DV), f32) * 0.5,
        'meta_tokens': nrm(ks[4], (N_META, D_MODEL), f32),
        'g_mix': 1.0 + 0.02 * nrm(ks[5], (DEPTH, D_MODEL), f32),
        'w_in': nrm(ks[6], (DEPTH, D_MODEL, N_IN), f32) * D_MODEL ** -0.5,
        'w_pool': nrm(ks[7], (DEPTH, POOL_GROUPS, POOL_GROUP, POOL_GROUP), f32) * POOL_GROUP ** -0.5,
        'pool_scale': 1.0 + 0.02 * nrm(ks[8], (DEPTH, POOL_WIDTH), f32),
        'hgrn_lb_logits': 0.5 * nrm(ks[9], (DEPTH + 1, HG_KTOT), f32),
        'g_onorm': 1.0 + 0.02 * nrm(ks[10], (DEPTH, HG_DV), f32),
        'w_a': nrm(ks[11], (DEPTH, POOL_WIDTH, D_MODEL), f32) * POOL_WIDTH ** -0.5,
        'w_b': nrm(ks[12], (DEPTH, HG_WIDTH, D_MODEL), f32) * HG_WIDTH ** -0.5,
        'w_out': nrm(ks[13], (DEPTH, D_MODEL, D_MODEL), f32) * D_MODEL ** -0.5,
        'g_mlp': 1.0 + 0.02 * nrm(ks[14], (DEPTH, D_MODEL), f32),
        'w_up': nrm(ks[15], (DEPTH, D_MODEL, D_FF), f32) * D_MODEL ** -0.5,
        'w_down': nrm(ks[16], (DEPTH, D_FF, D_MODEL), f32) * D_FF ** -0.5,
        'g_final': 1.0 + 0.02 * nrm(ks[17], (D_MODEL,), f32),
    }


def reference(x_prompt, x_sample, state_pool, state_hgrn, meta_tokens, g_mix, w_in, w_pool,
              pool_scale, hgrn_lb_logits, g_onorm, w_a, w_b, w_out, g_mlp, w_up, w_down, g_final):
    B = x_prompt.shape[0]
    lb_all = jnp.cumsum(jax.nn.softmax(hgrn_lb_logits.astype(jnp.float32), axis=0), axis=0)
    meta = jnp.broadcast_to(meta_tokens[None].astype(x_prompt.dtype), (B, N_META, D_MODEL))
    xp = jnp.concatenate([meta, x_prompt], axis=1)
    xs = x_sample
    pool_p, hgrn_p, pool_s, hgrn_s = [], [], [], []
    for l in range(DEPTH):
        params = (lb_all[l], g_mix[l], w_in[l], w_pool[l], pool_scale[l], g_onorm[l],
                  w_a[l], w_b[l], w_out[l], g_mlp[l], w_up[l], w_down[l])
        S_init = jnp.zeros((B, HG_HEADS, HG_DK, HG_DV), jnp.float32)
        xp, pp, sp = _layer(xp, None, S_init, 0, True, *params)
        xs, ps, ss = _layer(xs, state_pool[l], state_hgrn[l], PAST_LEN, False, *params)
        pool_p.append(pp)
        hgrn_p.append(sp)
        pool_s.append(ps)
        hgrn_s.append(ss)
    y_prompt = _rmsnorm(xp, g_final)[:, N_META:]
    y_sample = _rmsnorm(xs, g_final)
    new_pool_prompt = jnp.stack(pool_p)
    new_hgrn_prompt = jnp.stack(hgrn_p)
    new_pool_sample = jnp.stack(pool_s)
    new_hgrn_sample = jnp.stack(hgrn_s)
    return (y_prompt, y_sample, new_pool_prompt, new_hgrn_prompt, new_pool_sample, new_hgrn_sample)
```

```python
from contextlib import ExitStack
import numpy as np
import ml_dtypes
import concourse.bass as bass
import concourse.mybir as mybir
from concourse.bass_utils import run_bass_kernel_spmd

F32 = mybir.dt.float32
BF16 = mybir.dt.bfloat16
AF = mybir.ActivationFunctionType
ALU = mybir.AluOpType

ENGS = ("pe", "act", "dve", "pool", "sp")
EPS = 1e-6
PACE = True
SKEW = 6
CASTG = 1
NCORES = 8


class Buf:
    __slots__ = ("name", "w", "r")

    def __init__(self, name):
        self.name = name
        self.w = None
        self.r = []


class Op:
    __slots__ = ("eng", "emit", "deps", "sig", "dma", "dsem", "dval", "sigcnt", "prev_same_sem")

    def __init__(self, eng, emit, dma):
        self.eng = eng
        self.emit = emit
        self.dma = dma
        self.deps = []
        self.sig = False
        self.dsem = None
        self.dval = 0
        self.sigcnt = 0
        self.prev_same_sem = None


class Prog:
    def __init__(self, n_dma_sems=24, same_engine_sync=True):
        self.ops = {e: [] for e in ENGS}
        self.n_dma_sems = n_dma_sems
        self.same_engine_sync = same_engine_sync
        self.dma_count = {e: 0 for e in ENGS}
        self.dma_last = {}
        self.out_dmas = []

    def add(self, eng, emit, reads=(), writes=(), dma=False, is_output=False):
        op = Op(eng, emit, dma)
        deps = []
        seen = set()

        def push(d):
            if d is None or d is op or id(d) in seen:
                return
            seen.add(id(d))
            deps.append(d)

        for b in reads:
            push(b.w)
        for b in writes:
            push(b.w)
            for r in b.r:
                push(r)
        fdeps = []
        for d in deps:
            if (not d.dma) and (not dma) and d.eng == eng:
                if eng == "pe" or not self.same_engine_sync:
                    continue
            fdeps.append(d)
        op.deps = fdeps
        for d in fdeps:
            if not d.dma:
                d.sig = True
        for b in reads:
            b.r.append(op)
        for b in writes:
            b.w = op
            b.r = []
        if dma:
            slot = self.dma_count[eng] % self.n_dma_sems
            k = self.dma_count[eng] // self.n_dma_sems
            self.dma_count[eng] += 1
            op.dsem = (eng, slot)
            op.dval = 16 * (k + 1)
            op.prev_same_sem = self.dma_last.get((eng, slot))
            self.dma_last[(eng, slot)] = op
            if is_output:
                self.out_dmas.append(op)
        self.ops[eng].append(op)
        return op

    def emit_all(self, nc, es):
        esem = {e: es.enter_context(nc.semaphore(f"e_{e}")) for e in ENGS}
        dsem = {}
        for e in ENGS:
            n = min(self.dma_count[e], self.n_dma_sems)
            for s in range(n):
                dsem[(e, s)] = es.enter_context(nc.semaphore(f"d_{e}_{s}"))
        for e in ENGS:
            c = 0
            for op in self.ops[e]:
                if op.sig and not op.dma:
                    c += 1
                    op.sigcnt = c
        block = es.enter_context(nc.Block())
        out_dmas = self.out_dmas

        def run_engine(e, h):
            waited = {}

            def wait(semkey, sem, val):
                if waited.get(semkey, 0) >= val:
                    return
                h.wait_ge(sem, val)
                waited[semkey] = val

            for op in self.ops[e]:
                for d in op.deps:
                    if d.dma:
                        wait(("d",) + d.dsem, dsem[d.dsem], d.dval)
                    else:
                        wait(("e", d.eng), esem[d.eng], d.sigcnt)
                if op.dma and op.prev_same_sem is not None:
                    p = op.prev_same_sem
                    wait(("d",) + p.dsem, dsem[p.dsem], p.dval)
                inst = op.emit(h)
                if op.dma:
                    inst.then_inc(dsem[op.dsem], 16)
                elif op.sig:
                    inst.then_inc(esem[e], 1)
            if e == "sp":
                for d in out_dmas:
                    wait(("d",) + d.dsem, dsem[d.dsem], d.dval)

        @block.tensor
        def _(h):
            run_engine("pe", h)

        @block.scalar
        def _(h):
            run_engine("act", h)

        @block.vector
        def _(h):
            run_engine("dve", h)

        @block.gpsimd
        def _(h):
            run_engine("pool", h)

        @block.sync
        def _(h):
            run_engine("sp", h)


def _consts():
    c = {}
    s = np.arange(128)[:, None]
    t = np.arange(128)[None, :]
    c["ident_f"] = np.eye(128, dtype=np.float32)
    c["ident_b"] = np.eye(128).astype(ml_dtypes.bfloat16)
    tri2 = ((s <= t) & (s // 64 == t // 64)).astype(np.float32)
    triS = np.where(t < 16, (s <= t), (s == t)).astype(np.float32)
    maskS = ((t < 16) & (s <= t)).astype(np.float32)
    c["tri"] = np.stack([tri2, triS, tri2, maskS], 1).astype(np.float32)
    ci = np.zeros((128, 4), np.float32)
    ci[:64, 0] = 1
    ci[64:, 1] = 1
    ci[:16, 2] = 1
    c["chunkind"] = ci
    pm = np.zeros((128, 4, 128), np.float32)
    pp = np.zeros((128, 4, 128), np.float32)
    ppm = np.zeros((16, 4, 128), np.float32)
    psp = np.zeros((48, 4, 48), np.float32)
    psel = np.zeros((120, 4, 8), np.float32)
    for g, w in enumerate((2, 4, 8, 16)):
        for tt in range(128):
            for ss in range(tt - w + 1, tt + 1):
                if ss >= 0:
                    pm[ss, g, tt] += 1.0 / w
                else:
                    pp[128 + ss, g, tt] += 1.0 / w
                    if 16 + ss >= 0:
                        ppm[16 + ss, g, tt] += 1.0 / w
            pm[tt, g, tt] -= 1.0
        for tt in range(16):
            cnt = min(tt + 1, w)
            for ss in range(max(0, tt - w + 1), tt + 1):
                psp[ss, g, tt] += 1.0 / cnt
            psp[tt, g, tt] -= 1.0
        for b in range(16):
            psp[32 + b, g, 32 + b] = 1.0 / w - 1.0
        for b in range(8):
            for r in range(15):
                if r - 15 > -w:
                    psel[b * 15 + r, g, b] = 1.0 / w
    c["pmain"] = pm
    c["pprev"] = pp
    c["pprevmeta"] = ppm
    c["pspec"] = psp
    c["psel"] = psel
    eb = np.zeros((48, 16, 128), np.float32)
    for b in range(16):
        eb[32 + b, b, :] = 1.0
    c["eb"] = eb.astype(ml_dtypes.bfloat16)
    return c


_CONST_SPECS = [
    ("ident_f", [128, 128], F32), ("ident_b", [128, 128], BF16), ("tri", [128, 4, 128], F32),
    ("chunkind", [128, 4], F32), ("pmain", [128, 4, 128], F32), ("pprev", [128, 4, 128], F32),
    ("pprevmeta", [16, 4, 128], F32), ("pspec", [48, 4, 48], F32), ("psel", [120, 4, 8], F32),
    ("eb", [48, 16, 128], BF16),
]


def build_program(debug=False):
    nc = bass.Bass("TRN2", target_bir_lowering=False, dynamic_dma_scratch_size=4096)
    P = Prog()
    dbg_outs = {}

    def din(name, shape, dt=F32):
        return nc.dram_tensor(name, shape, dt, kind="ExternalInput").ap()

    def dout(name, shape, dt=F32):
        return nc.dram_tensor(name, shape, dt, kind="ExternalOutput").ap()

    xp = din("xp", [2048, 1024])
    xs = din("xs", [16, 1024])
    stp = din("stp", [240, 512])
    sth = din("sth", [16, 4, 128, 128])
    meta = din("meta", [16, 1024])
    w_in = din("w_in", [1024, 4608])
    w_pool = din("w_pool", [128, 4, 128])
    w_a = din("w_a", [512, 1024])
    w_b = din("w_b", [512, 1024])
    w_out = din("w_out", [1024, 1024])
    w_up = din("w_up", [1024, 4096])
    w_down = din("w_down", [4096, 1024])
    g_mixT = din("g_mixT", [128, 8])
    g_mlpT = din("g_mlpT", [128, 8])
    pscT = din("pscT", [128, 4])
    lbl = din("lbl", [2, 512])
    g_on = din("g_on", [1, 128])
    g_fin = din("g_fin", [1, 1024])
    cd = {n: din("c_" + n, sh, dt) for n, sh, dt in _CONST_SPECS}

    y_p = dout("y_p", [2048, 1024])
    y_s = dout("y_s", [16, 1024])
    npp = dout("npp", [15, 512])
    nhp = dout("nhp", [4, 128, 128])
    nps = dout("nps", [16, 15, 512])
    nhs = dout("nhs", [16, 4, 128, 128])

    def scratch(name, shape):
        return nc.dram_tensor(name, shape, BF16, kind="Internal").ap()

    s_in = scratch("s_in", [1024, 4608])
    s_a = scratch("s_a", [512, 1024])
    s_b = scratch("s_b", [512, 1024])
    s_out = scratch("s_out", [1024, 1024])
    s_up = scratch("s_up", [1024, 4096])
    s_down = scratch("s_down", [4096, 1024])

    with ExitStack() as es:
        def sb(name, shape, dt=F32):
            return es.enter_context(nc.sbuf_tensor(name, shape, dt))

        def emit_casts(first_deps):
            for k in range(8):
                P.add("pool", lambda h, k=k: h.dma_start(out=s_in[k * 128:(k + 1) * 128, 0:2560], in_=w_in[k * 128:(k + 1) * 128, 0:2560], max_dma_last_dim=4096),
                      reads=(first_deps if k == 0 else []), writes=[cb_in[k]], dma=True)

        def emit_casts2(deps):
            for k in range(8):
                P.add("pool", lambda h, k=k: h.dma_start(out=s_in[k * 128:(k + 1) * 128, 2560:4608], in_=w_in[k * 128:(k + 1) * 128, 2560:4608], max_dma_last_dim=4096),
                      reads=(deps if k == 0 else []), writes=[cb_in2[k]], dma=True)
            P.add("pool", lambda h: h.dma_start(out=s_a, in_=w_a, max_dma_last_dim=4096), writes=[cb_a], dma=True)
            P.add("pool", lambda h: h.dma_start(out=s_b, in_=w_b, max_dma_last_dim=4096), writes=[cb_b], dma=True)
            for k in range(2):
                P.add("pool", lambda h, k=k: h.dma_start(out=s_out[k * 512:(k + 1) * 512, :], in_=w_out[k * 512:(k + 1) * 512, :], max_dma_last_dim=4096), writes=[cb_out[k]], dma=True)

        def cast_up(k, deps):
            P.add("pool", lambda h: h.dma_start(out=s_up[k * 256:(k + 1) * 256, :], in_=w_up[k * 256:(k + 1) * 256, :], max_dma_last_dim=4096), reads=deps, writes=[cb_up[k]], dma=True)

        def cast_dn(k, deps):
            P.add("pool", lambda h: h.dma_start(out=s_down[k * 512:(k + 1) * 512, :], in_=w_down[k * 512:(k + 1) * 512, :], max_dma_last_dim=4096), reads=deps, writes=[cb_dn[k]], dma=True)
        cb_in = [Buf(f"cin{k}") for k in range(8)]
        cb_in2 = [Buf(f"cin2{k}") for k in range(8)]
        cb_a, cb_b = Buf("ca"), Buf("cb")
        cb_out = [Buf(f"cout{k}") for k in range(2)]
        cb_up = [Buf(f"cup{k}") for k in range(4)]
        cb_dn = [Buf(f"cdn{k}") for k in range(8)]

        aT = sb("aT", [128, 32, 512], BF16)
        b_aT = Buf("aT")
        b_aTf = [Buf(f"aTf{i}") for i in range(32)]
        csb, cbuf = {}, {}
        for n, sh, dt in _CONST_SPECS:
            cbuf[n] = Buf("k_" + n)
            if n == "eb":
                csb[n] = aT[0:48, 14:18, :].rearrange("p a (b c) -> p (a b) c", c=128)
                P.add("sp", lambda h, n=n: h.dma_start(out=csb[n], in_=cd[n]), reads=[b_aT], writes=[cbuf[n]], dma=True)
            else:
                csb[n] = sb("k_" + n, sh, dt)
                P.add("sp", lambda h, n=n: h.dma_start(out=csb[n][:], in_=cd[n]), writes=[cbuf[n]], dma=True)

        def small(name, shape, src, dt=F32):
            t = sb(name, shape, dt)
            b = Buf(name)
            P.add("sp", lambda h: h.dma_start(out=t[:], in_=src), writes=[b], dma=True)
            return t, b

        gmix_sb, b_gmix = small("gmix", [128, 8], g_mixT)
        gmlp_sb, b_gmlp = small("gmlp", [128, 8], g_mlpT)
        psc_sb, b_psc = small("psc", [128, 4], pscT)
        gfin_sb, b_gfin = small("gfin", [128, 1024], g_fin.to_broadcast([128, 1024]))
        l0_sb, b_l0 = small("l0", [128, 512], lbl[0:1, :].to_broadcast([128, 512]))
        l1_sb, b_l1 = small("l1", [128, 512], lbl[1:2, :].to_broadcast([128, 512]))
        gon_sb = sb("gon", [128, 4, 128])
        b_gon = Buf("gon")
        for hh in range(4):
            P.add("sp", lambda h, hh=hh: h.dma_start(out=gon_sb[:, hh, :], in_=g_on.to_broadcast([128, 128])), writes=[b_gon], dma=True)
        rtmp = sb("rtmp", [128, 512]); b_rtmp = Buf("rtmp")
        wpool_f = rtmp[:].rearrange("p (a b) -> p a b", a=4)
        b_wpf = b_rtmp
        P.add("sp", lambda h: h.dma_start(out=wpool_f, in_=w_pool), writes=[b_wpf], dma=True)
        wpool_sb = sb("wpool", [128, 4, 128], BF16)
        b_wp = Buf("wp")
        P.add("dve", lambda h: h.tensor_copy(out=wpool_sb[:], in_=wpool_f), reads=[b_wpf], writes=[b_wp])
        stp_sb = aT[0:120, 10:14, :].rearrange("p a b -> p (a b)").bitcast(F32).rearrange("p (j c) -> p j c", j=2)
        b_stp = Buf("stp")
        for j in range(2):
            P.add("sp", lambda h, j=j: h.dma_start(out=stp_sb[:, j, :], in_=stp[j * 120:(j + 1) * 120, :]), reads=[b_aT], writes=[b_stp], dma=True)

        eps_sb = sb("eps", [128, 1])
        b_eps = Buf("eps")
        P.add("dve", lambda h: h.memset(eps_sb[:], EPS), writes=[b_eps])
        lb_sb = l0_sb
        oml_sb = l1_sb
        b_lb, b_oml = b_l0, b_l1
        P.add("dve", lambda h: h.tensor_tensor(out=lb_sb[:], in0=l0_sb[:], in1=l1_sb[:], op=ALU.subtract), reads=[b_l0, b_l1], writes=[b_lb])
        P.add("act", lambda h: h.activation(out=lb_sb[:], in_=lb_sb[:], func=AF.Sigmoid), reads=[b_lb], writes=[b_lb])
        P.add("dve", lambda h: h.tensor_scalar(out=oml_sb[:], in0=lb_sb[:], scalar1=-1.0, scalar2=1.0, op0=ALU.mult, op1=ALU.add), reads=[b_lb], writes=[b_oml])

        if not PACE:
            for k in range(4):
                cast_up(k, [])
            for k in range(8):
                cast_dn(k, [])
        first_cast_deps = [[cbuf[n] for n, _, _ in _CONST_SPECS] + [b_gmix, b_gmlp, b_psc, b_gfin, b_l0, b_l1, b_gon, b_wpf, b_stp]]
        NS = 20
        ring = sb("ring", [128, NS, 1024], BF16)
        ring_b = [Buf(f"ring{i}") for i in range(NS)]
        ring_pos = [0]

        def wload_cast(src32, width, deps=()):
            i = ring_pos[0] % NS
            ring_pos[0] += 1
            P.add("pool", lambda h: h.dma_start(out=ring[:, i, 0:width], in_=src32, max_dma_last_dim=4096), reads=list(deps), writes=[ring_b[i]], dma=True)
            return ring[:, i, :], ring_b[i]

        def wload(src, width, dep):
            i = ring_pos[0] % NS
            ring_pos[0] += 1
            P.add("sp", lambda h: h.dma_start(out=ring[:, i, 0:width], in_=src), reads=[dep], writes=[ring_b[i]], dma=True)
            return ring[:, i, :], ring_b[i]

        xg_n = sb("xg", [128, 4, 1024])
        b_xg_n = [Buf(f"xg{i}") for i in range(4)]
        hT_nb = [sb("hT", [128, 8, 512], BF16), sb("hT2", [128, 8, 512], BF16)]
        b_hT_nb = [Buf("hT"), Buf("hT2")]
        xstage = sb("xstage", [128, 1024])
        b_xstage = Buf("xstage")
        xg_s = sb("xg_s", [128, 1, 1024])
        b_xg_s = [Buf("xg_s")]
        hT_s = sb("hT_s", [128, 8, 48], BF16)
        b_hT_s = Buf("hT_s")
        obT_s = sb("obT_s", [128, 4, 48], BF16)
        b_obT_s = Buf("obT_s")
        paT_s = sb("paT_s", [128, 4, 48], BF16)
        b_paT_s = Buf("paT_s")
        xn = sb("xn", [128, 1024], BF16)
        b_xn = Buf("xn")
        junk = rtmp[:].bitcast(BF16)
        b_junk = b_rtmp
        st = sb("st", [128, 16])
        b_st = Buf("st")
        u_g_n = sb("u_g", [128, 4, 512])
        b_u_n = [Buf(f"u{i}") for i in range(4)]
        u_prev = sb("u_prev", [128, 512])
        b_uprev = Buf("uprev")
        sq_g_n = sb("sq_g", [128, 4, 512])
        b_sq_n = [Buf(f"sq{i}") for i in range(4)]
        fg_g_n = sb("fg_g", [128, 4, 512])
        b_fg_n = [Buf(f"fg{i}") for i in range(4)]
        v_g_n = sb("v_g", [128, 4, 512], BF16)
        b_v_n = [Buf(f"v{i}") for i in range(4)]
        sog_g_n = sb("sog_g", [128, 4, 512], BF16)
        b_sog_n = [Buf(f"sog{i}") for i in range(4)]
        sga_g = sb("sga_g", [128, 4, 1024], BF16)
        b_sga = [Buf(f"sga{i}") for i in range(4)]
        obT_n = sb("obT", [128, 4, 512], BF16)
        b_obT_n = Buf("obT")
        paT_n = sb("paT", [128, 4, 512], BF16)
        b_paT_n = Buf("paT")
        mT = sb("mT", [128, 8, 512], BF16)
        b_mT = Buf("mT")
        g_t = sb("g_t", [128, 512]); b_g = Buf("g_t")
        k_t = sb("k_t", [128, 512]); b_k = Buf("k_t")
        Ep = sb("Ep", [128, 512]); b_Ep = Buf("Ep")
        En = sb("En", [128, 512]); b_En = Buf("En")
        dG2 = sb("dG2", [128, 2, 8]); b_dG2 = [Buf("dG2a"), Buf("dG2b")]
        qt = sb("qt", [128, 512], BF16); b_qt = Buf("qt")
        kt = sb("kt", [128, 512], BF16); b_kt = Buf("kt")
        qkT = sb("qkT", [128, 8, 128], BF16); b_qkT = Buf("qkT")
        AT = sb("AT", [128, 4, 128], BF16); b_AT = Buf("AT")
        S = sb("S", [128, 4, 128]); b_S = Buf("S")
        S_bf = sb("S_bf", [128, 4, 128], BF16); b_Sbf = Buf("S_bf")
        T1 = sb("T1", [128, 4, 128]); b_T1 = Buf("T1")
        ob1 = aT[:, 30:32, :].rearrange("p a b -> p (a b)").bitcast(F32); b_ob1 = Buf("ob1")
        ob = sb("ob", [128, 512], BF16); b_ob = Buf("ob")
        pooled = sb("pooled", [128, 4, 128], BF16); b_pooled = Buf("pooled")
        t1 = g_t; b_t1 = b_g
        t2 = k_t; b_t2 = b_k
        m_t = En[:].bitcast(BF16); b_m = b_En
        m_tb = [En[:].bitcast(BF16), Ep[:].bitcast(BF16)]; b_mb = [b_En, b_Ep]
        sfT = sb("sfT", [128, 12, 16]); b_sfT = Buf("sfT")
        sqTb = sb("sqTb", [128, 4, 16], BF16); b_sqTb = Buf("sqTb")
        s0buf = aT[:, 0:4, :].rearrange("p a b -> p (a b)").bitcast(F32).rearrange("p (s a v) -> p s a v", s=2, a=4); b_s0 = [Buf("s0a"), Buf("s0b")]
        snew = aT[:, 4:8, :].rearrange("p a b -> p (a b)").bitcast(F32).rearrange("p (s a v) -> p s a v", s=2, a=4); b_sn = [Buf("sna"), Buf("snb")]
        snbf = aT[:, 8:10, :].rearrange("p a b -> p (a b)").rearrange("p (s a v) -> p s a v", s=2, a=4); b_snbf = [Buf("snbfa"), Buf("snbfb")]
        osT = sb("osT", [128, 4, 16]); b_osT = Buf("osT")
        vs_s = sb("vs_s", [128, 512], BF16); b_vs = Buf("vs_s")
        sog_s = sb("sog_s", [128, 512], BF16); b_sogs = Buf("sog_s")
        s0b2 = aT[:, 0:8, :].rearrange("p a b -> p (a b)").bitcast(F32).rearrange("p (q s a v) -> p q s a v", q=2, s=2, a=4)
        snewb2 = aT[:, 22:30, :].rearrange("p a b -> p (a b)").bitcast(F32).rearrange("p (q s a v) -> p q s a v", q=2, s=2, a=4)
        snbfb2 = aT[:, 18:22, :].rearrange("p a b -> p (a b)").rearrange("p (q s a v) -> p q s a v", q=2, s=2, a=4)
        def _arena32(a, b):
            return aT[:, a:b, :].rearrange("p a b -> p (a b)").bitcast(F32).rearrange("p (t c) -> p t c", t=1)
        u_gs, sq_gs, fg_gs = _arena32(18, 20), _arena32(20, 22), _arena32(22, 24)
        v_gs = aT[:, 24:25, :]
        sog_gs = aT[:, 25:26, :]
        b_u_s, b_sq_s, b_fg_s, b_v_s, b_sog_s = [Buf("u_s")], [Buf("sq_s")], [Buf("fg_s")], [Buf("v_s")], [Buf("sog_s")]
        NBANK = [6]
        s0b = aT[:, 0:8, :].rearrange("p a b -> p (a b)").bitcast(F32).rearrange("p (s a v) -> p s a v", s=4, a=4); b_s0b = Buf("s0b")
        snewb = aT[:, 22:30, :].rearrange("p a b -> p (a b)").bitcast(F32).rearrange("p (s a v) -> p s a v", s=4, a=4); b_snewb = Buf("snewb")
        snbfb = aT[:, 18:22, :].rearrange("p a b -> p (a b)").rearrange("p (s a v) -> p s a v", s=4, a=4); b_snbfb = Buf("snbfb")

        pb = [es.enter_context(nc.psum_tensor(f"pb{i}", [128, 512], F32)) for i in range(8)]
        b_pb = [Buf(f"pb{i}") for i in range(8)]
        psT = pb[7][:].bitcast(BF16).rearrange("p (k n) -> p k n", k=8)

        P.add("dve", lambda h: h.memset(S[:], 0.0), writes=[b_S])
        P.add("dve", lambda h: h.memset(S_bf[:], 0.0), writes=[b_Sbf])

        def dump(name, ap, bufs, rows, cols):
            if not debug:
                return
            d = nc.dram_tensor("dbg_" + name, [rows, cols], F32, kind="ExternalOutput").ap()
            if "stg" not in dbg_outs:
                dbg_outs["stg"] = (sb("dstg", [128, 512]), Buf("dstg"))
            stg, bst = dbg_outs["stg"]
            P.add("dve", lambda h: h.tensor_copy(out=stg[0:rows, :], in_=ap), reads=bufs, writes=[bst])
            P.add("sp", lambda h: h.dma_start(out=d, in_=stg[0:rows, :]), reads=[bst], dma=True, is_output=True)

        def rms_to_T(src_ap_fn, b_src, rows, gT_sb, b_gT, col0, hT, b_hT, defer=None):
            P.add("act", lambda h: h.activation(out=junk[0:rows, :], in_=src_ap_fn(), func=AF.Square, scale=1.0 / 32.0, accum_out=st[0:rows, 0:1]),
                  reads=[b_src], writes=[b_junk, b_st])
            P.add("act", lambda h: h.activation(out=st[0:rows, 1:2], in_=st[0:rows, 0:1], func=AF.Ln, bias=eps_sb[0:rows, :], scale=1.0),
                  reads=[b_st, b_eps], writes=[b_st])
            P.add("act", lambda h: h.activation(out=st[0:rows, 2:3], in_=st[0:rows, 1:2], func=AF.Exp, scale=-0.5), reads=[b_st], writes=[b_st])
            P.add("act", lambda h: h.activation(out=xn[0:rows, :], in_=src_ap_fn(), func=AF.Copy, scale=st[0:rows, 2:3]), reads=[b_src, b_st], writes=[b_xn])

            def pe_part():
                def tr(h):
                    for k in range(8):
                        i = h.transpose(out=psT[:, k, 0:rows], in_=xn[0:rows, k * 128:(k + 1) * 128], identity=csb["ident_b"][0:rows, 0:rows])
                    return i
                P.add("pe", tr, reads=[b_xn, cbuf["ident_b"]], writes=[b_pb[7]])
                P.add("dve", lambda h: h.tensor_tensor(out=hT[:, :, col0:col0 + rows], in0=psT[:, :, 0:rows],
                                                       in1=gT_sb[:].unsqueeze(2).to_broadcast([128, 8, rows]), op=ALU.mult),
                      reads=[b_pb[7], b_gT], writes=[b_hT])
            if defer is None:
                pe_part()
            else:
                defer.append(pe_part)

        bank_rr = [0]

        BANKSET = [None]

        def next_bank(nb=None):
            if BANKSET[0] is not None:
                bank_rr[0] += 1
                return BANKSET[0][bank_rr[0] % len(BANKSET[0])]
            nb = nb or NBANK[0]
            i = bank_rr[0] % nb
            bank_rr[0] += 1
            return i

        def do_group(gi, tiles, special, part="AB", defer=None, nxt=None, sink=None, pre_stages=None):
            nt = len(tiles)
            T = tiles[-1][1] + tiles[-1][0]
            if special:
                xg, b_xg, hT, b_hT, obT, b_obT, paT, b_paT = xg_s, b_xg_s, hT_s, b_hT_s, obT_s, b_obT_s, paT_s, b_paT_s
                u_g, sq_g, fg_g, v_g, sog_g = u_gs, sq_gs, fg_gs, v_gs, sog_gs
                b_u, b_sq, b_fg, b_v, b_sog = b_u_s, b_sq_s, b_fg_s, b_v_s, b_sog_s
            else:
                xg, b_xg, hT, b_hT, obT, b_obT, paT, b_paT = xg_n, b_xg_n, hT_nb[gi % 2], b_hT_nb[gi % 2], obT_n, b_obT_n, paT_n, b_paT_n
                u_g, sq_g, fg_g, v_g, sog_g = u_g_n, sq_g_n, fg_g_n, v_g_n, sog_g_n
                b_u, b_sq, b_fg, b_v, b_sog = b_u_n, b_sq_n, b_fg_n, b_v_n, b_sog_n
            def win_block(c0, width, evac, after=None):
                if c0 < 2560 and (gi <= 0) and part != "B":
                    units = []
                    for k in range(8):
                        d = first_cast_deps.pop() if first_cast_deps else ()
                        units.append(wload_cast(w_in[k * 128:(k + 1) * 128, c0:c0 + width], width, d))
                else:
                    units = [wload(s_in[k * 128:(k + 1) * 128, c0:c0 + width], width, (cb_in if c0 < 2560 else cb_in2)[k]) for k in range(8)]
                for ti, (rows, col0) in enumerate(tiles):
                    for half in range(width // 512):
                        bk = next_bank()

                        def mm(h, rows=rows, col0=col0, half=half, bk=bk):
                            for k in range(8):
                                i = h.matmul(pb[bk][0:rows, :], lhsT=hT[:, k, col0:col0 + rows], rhs=units[k][0][:, half * 512:(half + 1) * 512],
                                             start=(k == 0), stop=(k == 7))
                            return i
                        P.add("pe", mm, reads=[b_hT] + [u[1] for u in units], writes=[b_pb[bk]])
                        evac(ti, rows, half, bk)
                        if after is not None:
                            after()
                return units

            def part_A1():
                if special:
                    P.add("dve", lambda h: h.memset(xg[0:48, 0, :], 0.0), writes=[b_xg[0]])
                    P.add("sp", lambda h: h.dma_start(out=xg[0:16, 0, :], in_=meta), writes=[b_xg[0]], dma=True)
                    P.add("sp", lambda h: h.dma_start(out=xg[32:48, 0, :], in_=xs), writes=[b_xg[0]], dma=True)
                    rms_to_T(lambda: xg[0:48, 0, :], b_xg[0], 48, gmix_sb, b_gmix, 0, hT, b_hT)
                else:
                    for ti, (rows, col0) in enumerate(tiles):
                        def pre(ti=ti, rows=rows, col0=col0):
                            r0 = (gi * 4 + ti) * 128
                            P.add("sp", lambda h: h.dma_start(out=xstage[:, :], in_=xp[r0:r0 + 128, :]), writes=[b_xstage], dma=True)
                            rms_to_T(lambda: xstage[:, :], b_xstage, rows, gmix_sb, b_gmix, col0, hT, b_hT, defer=(None if defer is None else defer["pe"]))
                        if defer is None:
                            pre()
                        else:
                            defer["pre"].append(pre)

            def part_A():
                def evA0(ti, rows, half, bk):
                    if half == 0:
                        P.add("act", lambda h: h.activation(out=u_g[0:rows, ti, :], in_=pb[bk][0:rows, :], func=AF.Copy), reads=[b_pb[bk]], writes=[b_u[ti]])
                    else:
                        P.add("act", lambda h: h.activation(out=sq_g[0:rows, ti, :], in_=pb[bk][0:rows, :], func=AF.Silu), reads=[b_pb[bk]], writes=[b_sq[ti]])

                def evA1(ti, rows, half, bk):
                    if half == 0:
                        P.add("act", lambda h: h.activation(out=fg_g[0:rows, ti, :], in_=pb[bk][0:rows, :], func=AF.Sigmoid), reads=[b_pb[bk]], writes=[b_fg[ti]])
                        P.add("dve", lambda h: h.tensor_tensor(out=fg_g[0:rows, ti, :], in0=fg_g[0:rows, ti, :], in1=oml_sb[0:rows, :], op=ALU.mult),
                              reads=[b_fg[ti], b_oml], writes=[b_fg[ti]])
                        P.add("dve", lambda h: h.tensor_tensor(out=fg_g[0:rows, ti, :], in0=fg_g[0:rows, ti, :], in1=lb_sb[0:rows, :], op=ALU.add),
                              reads=[b_fg[ti], b_lb], writes=[b_fg[ti]])
                    else:
                        P.add("act", lambda h: h.activation(out=v_g[0:rows, ti, :], in_=pb[bk][0:rows, :], func=AF.Copy), reads=[b_pb[bk]], writes=[b_v[ti]])

                def evA2(ti, rows, half, bk):
                    P.add("act", lambda h: h.activation(out=sog_g[0:rows, ti, :], in_=pb[bk][0:rows, :], func=AF.Silu), reads=[b_pb[bk]], writes=[b_sog[ti]])
                    P.add("dve", lambda h: h.tensor_tensor(out=sog_g[0:rows, ti, :], in0=sog_g[0:rows, ti, :],
                                                           in1=gon_sb[0:rows, :, :].rearrange("p a b -> p (a b)"), op=ALU.mult),
                          reads=[b_sog[ti], b_gon], writes=[b_sog[ti]])

                def hk(ti=None, half=None):
                    if pre_stages:
                        pre_stages.pop(0)()
                if pre_stages:
                    BANKSET[0] = [3, 5]
                win_block(0, 1024, evA0, after=hk)
                win_block(1024, 1024, evA1, after=hk)
                uB2 = win_block(2048, 512, evA2, after=hk)
                if gi == 0 and not special:
                    emit_casts2([uB2[-1][1]])
                    emit_casts([])

                tsel = 1 if special else 0
                chunks = [(0, 16)] if special else [(0, 64), (64, 64)]
                ccol = 2 if special else 0
                msel = 3 if special else 2
                pa2 = pb[2][:].rearrange("p (a b) -> p a b", a=4)
                p6 = pb[6][:].rearrange("p (a b) -> p a b", a=4)
                p1 = pb[1][:].rearrange("p (a b) -> p a b", a=4)
                psT6 = pb[6][:].bitcast(BF16).rearrange("p (k n) -> p k n", k=8)

                def tile_stages(ti, r, col0):
                    par = ti % 2
                    dGp = dG2[:, par, :]
                    b_dGp = b_dG2[par]
                    st_ = []

                    def s0():
                        P.add("act", lambda h: h.activation(out=g_t[0:r, :], in_=fg_g[0:r, ti, :], func=AF.Ln), reads=[b_fg[ti]], writes=[b_g])
                        P.add("dve", lambda h: h.tensor_scalar(out=k_t[0:r, :], in0=fg_g[0:r, ti, :], scalar1=-1.0, scalar2=1.0, op0=ALU.mult, op1=ALU.add),
                              reads=[b_fg[ti]], writes=[b_k])
                    st_.append(s0)

                    def s1():
                        P.add("pe", lambda h: h.matmul(pb[0][0:r, :], lhsT=csb["tri"][0:r, tsel, 0:r], rhs=g_t[0:r, :], start=True, stop=True),
                              reads=[b_g, cbuf["tri"]], writes=[b_pb[0]])

                        def mm_dg(h):
                            for hh in range(4):
                                i = h.matmul(pb[1][:, hh * 2:hh * 2 + 2], lhsT=g_t[0:r, hh * 128:(hh + 1) * 128], rhs=csb["chunkind"][0:r, ccol:ccol + 2], start=True, stop=True)
                            return i
                        P.add("pe", mm_dg, reads=[b_g, cbuf["chunkind"]], writes=[b_pb[1]])
                    st_.append(s1)

                    def s2():
                        P.add("act", lambda h: h.activation(out=Ep[0:r, :], in_=pb[0][0:r, :], func=AF.Exp), reads=[b_pb[0]], writes=[b_Ep])
                        P.add("act", lambda h: h.activation(out=En[0:r, :], in_=pb[0][0:r, :], func=AF.Exp, scale=-1.0), reads=[b_pb[0]], writes=[b_En])
                        P.add("act", lambda h: h.activation(out=dGp, in_=pb[1][:, 0:8], func=AF.Exp), reads=[b_pb[1]], writes=[b_dGp])
                    st_.append(s2)

                    def s3():
                        P.add("dve", lambda h: h.tensor_tensor(out=qt[0:r, :], in0=sq_g[0:r, ti, :], in1=Ep[0:r, :], op=ALU.mult), reads=[b_sq[ti], b_Ep], writes=[b_qt])
                        P.add("dve", lambda h: h.tensor_tensor(out=kt[0:r, :], in0=k_t[0:r, :], in1=En[0:r, :], op=ALU.mult), reads=[b_k, b_En], writes=[b_kt])
                    st_.append(s3)

                    def s4():
                        def tr_qk(h):
                            for hh in range(4):
                                h.transpose(out=psT[:, hh, 0:r], in_=qt[0:r, hh * 128:(hh + 1) * 128], identity=csb["ident_b"][0:r, 0:r])
                            for hh in range(4):
                                i = h.transpose(out=psT[:, 4 + hh, 0:r], in_=kt[0:r, hh * 128:(hh + 1) * 128], identity=csb["ident_b"][0:r, 0:r])
                            return i
                        P.add("pe", tr_qk, reads=[b_qt, b_kt, cbuf["ident_b"]], writes=[b_pb[7]])
                    st_.append(s4)

                    def s5():
                        P.add("dve", lambda h: h.tensor_copy(out=qkT[:, :, 0:r], in_=psT[:, :, 0:r]), reads=[b_pb[7]], writes=[b_qkT])
                    st_.append(s5)

                    def s6():
                        def mm_A(h):
                            for hh in range(4):
                                i = h.matmul(pa2[0:r, hh, 0:r], lhsT=qkT[:, 4 + hh, 0:r], rhs=qkT[:, hh, 0:r], start=True, stop=True)
                            return i
                        P.add("pe", mm_A, reads=[b_qkT], writes=[b_pb[2]])

                        def mm_U(h):
                            for ci, (rs, cs) in enumerate(chunks):
                                for hh in range(4):
                                    i = h.matmul(pb[4 + ci][:, hh * 128:(hh + 1) * 128], lhsT=kt[rs:rs + cs, hh * 128:(hh + 1) * 128],
                                                 rhs=v_g[rs:rs + cs, ti, hh * 128:(hh + 1) * 128], start=True, stop=True)
                            return i
                        P.add("pe", mm_U, reads=[b_kt, b_v[ti]], writes=[b_pb[4 + ci] for ci in range(len(chunks))])
                    st_.append(s6)

                    def s7():
                        P.add("dve", lambda h: h.tensor_tensor(out=AT[0:r, :, 0:r], in0=pa2[0:r, :, 0:r],
                                                               in1=csb["tri"][0:r, msel:msel + 1, 0:r].to_broadcast([r, 4, r]), op=ALU.mult),
                              reads=[b_pb[2], cbuf["tri"]], writes=[b_AT])
                    st_.append(s7)

                    def mk_ox(ci, rs, cs, with_intra):
                        def sx():
                            if with_intra:
                                def mm_oi(h):
                                    for hh in range(4):
                                        i = h.matmul(pb[3][0:r, hh * 128:(hh + 1) * 128], lhsT=AT[0:r, hh, 0:r], rhs=v_g[0:r, ti, hh * 128:(hh + 1) * 128], start=(hh == 0), stop=False)
                                    return i
                                P.add("pe", mm_oi, reads=[b_AT, b_v[ti]], writes=[b_pb[3]])

                            def mm_ox(h):
                                for hh in range(4):
                                    i = h.matmul(pb[3][rs:rs + cs, hh * 128:(hh + 1) * 128], lhsT=qkT[:, hh, rs:rs + cs], rhs=S_bf[:, hh, :], start=False,
                                                 stop=(hh == 3))
                                return i
                            P.add("pe", mm_ox, reads=[b_qkT, b_Sbf], writes=[b_pb[3]])
                        return sx

                    def mk_upd(ci):
                        def su():
                            dgb = dGp.rearrange("p (a c) -> p a c", c=2)[:, :, ci:ci + 1].to_broadcast([128, 4, 128])
                            P.add("dve", lambda h: h.tensor_tensor(out=T1[:], in0=pb[4 + ci][:, :].rearrange("p (a b) -> p a b", a=4), in1=S[:], op=ALU.add),
                                  reads=[b_pb[4 + ci], b_S], writes=[b_T1])
                            P.add("dve", lambda h: h.tensor_tensor(out=S[:], in0=T1[:], in1=dgb, op=ALU.mult), reads=[b_T1, b_dGp], writes=[b_S])
                            P.add("act", lambda h: h.activation(out=S_bf[:], in_=S[:], func=AF.Copy), reads=[b_S], writes=[b_Sbf])
                        return su
                    if special:
                        st_.append(mk_upd(0))
                        st_.append(sample_save)
                    else:
                        for ci, (rs, cs) in enumerate(chunks):
                            st_.append(mk_ox(ci, rs, cs, ci == 0))
                            st_.append(mk_upd(ci))

                    def s12():
                        for hh in range(4):
                            P.add("act", lambda h, hh=hh: h.activation(out=junk[0:r, 0:128], in_=pb[3][0:r, hh * 128:(hh + 1) * 128], func=AF.Square,
                                                                      scale=float(1.0 / np.sqrt(128.0)), accum_out=st[0:r, 4 + hh:5 + hh]),
                                  reads=[b_pb[3]], writes=[b_junk, b_st])
                        P.add("act", lambda h: h.activation(out=st[0:r, 8:12], in_=st[0:r, 4:8], func=AF.Ln, bias=eps_sb[0:r, :], scale=1.0), reads=[b_st, b_eps], writes=[b_st])
                        P.add("act", lambda h: h.activation(out=st[0:r, 12:16], in_=st[0:r, 8:12], func=AF.Exp, scale=-0.5), reads=[b_st], writes=[b_st])
                        P.add("dve", lambda h: h.tensor_tensor(out=ob1[0:r, :].rearrange("p (a b) -> p a b", a=4), in0=pb[3][0:r, :].rearrange("p (a b) -> p a b", a=4),
                                                               in1=st[0:r, 12:16].unsqueeze(2).to_broadcast([r, 4, 128]), op=ALU.mult),
                              reads=[b_pb[3], b_st, b_aT], writes=[b_ob1])
                        P.add("dve", lambda h: h.tensor_tensor(out=ob[0:r, :], in0=ob1[0:r, :], in1=sog_g[0:r, ti, :], op=ALU.mult), reads=[b_ob1, b_sog[ti], b_aT], writes=[b_ob])
                    if not special:
                        st_.append(s12)

                    def s13():
                        def tr_ob(h):
                            for hh in range(4):
                                i = h.transpose(out=psT6[:, hh, 0:r], in_=ob[0:r, hh * 128:(hh + 1) * 128], identity=csb["ident_b"][0:r, 0:r])
                            return i
                        P.add("pe", tr_ob, reads=[b_ob, cbuf["ident_b"]], writes=[b_pb[6]])
                    if not special:
                        st_.append(s13)

                    def s14():
                        P.add("act", lambda h: h.activation(out=obT[:, :, col0:col0 + r], in_=psT6[:, 0:4, 0:r], func=AF.Copy), reads=[b_pb[6]], writes=[b_obT])
                    if not special:
                        st_.append(s14)

                    def s15():
                        def mm_pool(h):
                            for gg in range(4):
                                if special:
                                    h.matmul(p6[:, gg, 0:48], lhsT=u_g[0:48, 0, gg * 128:(gg + 1) * 128], rhs=csb["pspec"][0:48, gg, :], start=True, stop=False)
                                    for j in range(2):
                                        i = h.matmul(p6[:, gg, 32 + 8 * j:40 + 8 * j], lhsT=stp_sb[:, j, gg * 128:(gg + 1) * 128], rhs=csb["psel"][:, gg, :], start=False, stop=(j == 1))
                                else:
                                    h.matmul(p6[:, gg, :], lhsT=u_g[:, ti, gg * 128:(gg + 1) * 128], rhs=csb["pmain"][:, gg, :], start=True, stop=False)
                                    if ti > 0:
                                        i = h.matmul(p6[:, gg, :], lhsT=u_g[:, ti - 1, gg * 128:(gg + 1) * 128], rhs=csb["pprev"][:, gg, :], start=False, stop=True)
                                    elif gi == 0:
                                        i = h.matmul(p6[:, gg, :], lhsT=u_prev[0:16, gg * 128:(gg + 1) * 128], rhs=csb["pprevmeta"][:, gg, :], start=False, stop=True)
                                    else:
                                        i = h.matmul(p6[:, gg, :], lhsT=u_prev[:, gg * 128:(gg + 1) * 128], rhs=csb["pprev"][:, gg, :], start=False, stop=True)
                            return i
                        rd = [b_u[ti], cbuf["pmain"], cbuf["pprev"], cbuf["pprevmeta"], cbuf["pspec"], cbuf["psel"]]
                        if special:
                            rd += [b_stp, b_aT]
                        rd.append(b_u[ti - 1] if ti > 0 else b_uprev)
                        P.add("pe", mm_pool, reads=rd, writes=[b_pb[6]])
                    st_.append(s15)

                    def s16():
                        P.add("act", lambda h: h.activation(out=pooled[:, :, 0:r], in_=p6[:, :, 0:r], func=AF.Copy), reads=[b_pb[6]], writes=[b_pooled])
                    st_.append(s16)

                    def s17():
                        def mm_wp(h):
                            for gg in range(4):
                                i = h.matmul(p1[:, gg, 0:r], lhsT=wpool_sb[:, gg, :], rhs=pooled[:, gg, 0:r], start=True, stop=True)
                            return i
                        P.add("pe", mm_wp, reads=[b_pooled, b_wp], writes=[b_pb[1]])
                    st_.append(s17)

                    def s18():
                        P.add("dve", lambda h: h.tensor_tensor(out=paT[:, :, col0:col0 + r], in0=p1[:, :, 0:r],
                                                               in1=psc_sb[:].unsqueeze(2).to_broadcast([128, 4, r]), op=ALU.mult),
                              reads=[b_pb[1], b_psc], writes=[b_paT])
                        if gi == CASTG and PACE:
                            cast_up(ti, [b_paT])
                    st_.append(s18)
                    return st_

                while pre_stages:
                    pre_stages.pop(0)()
                BANKSET[0] = None
                all_st = [tile_stages(ti, rows, col0) for ti, (rows, col0) in enumerate(tiles)]
                nst = len(all_st[0])
                SK = SKEW
                if sink is not None:
                    sink.extend(all_st[0])
                    sink.append(part_A_tail)
                    return
                for step in range(nst + SK * (nt - 1)):
                    for ti in range(nt):
                        k = step - SK * ti
                        if 0 <= k < nst:
                            all_st[ti][k]()
                part_A_tail()

            def part_A_tail():

                lt = nt - 1
                if special:
                    P.add("dve", lambda h: h.tensor_copy(out=u_prev[0:16, :], in_=u_g[0:16, 0, :]), reads=[b_u[0]], writes=[b_uprev])
                    P.add("pool", lambda h: h.dma_start(out=nps[:, 0:14, :], in_=stp.rearrange("(b r) c -> b r c", r=15)[:, 1:15, :]), dma=True, is_output=True)
                    P.add("pool", lambda h: h.dma_start(out=nps[:, 14, :], in_=u_g[32:48, 0, :]), reads=[b_u[0], b_aT], dma=True, is_output=True)
                else:
                    P.add("dve", lambda h: h.tensor_copy(out=u_prev[:, :], in_=u_g[:, lt, :]), reads=[b_u[lt]], writes=[b_uprev])
                    if gi == 3:
                        P.add("pool", lambda h: h.dma_start(out=npp, in_=u_g[113:128, 3, :]), reads=[b_u[3]], dma=True, is_output=True)
                        P.add("pool", lambda h: h.dma_start(out=nhp.rearrange("a k v -> k a v"), in_=S[:]), reads=[b_S], dma=True, is_output=True)

            def part_B():
                if not special:
                    for ti in range(nt):
                        r0 = (gi * 4 + ti) * 128
                        P.add("sp", lambda h, ti=ti, r0=r0: h.dma_start(out=xg[:, ti, :], in_=xp[r0:r0 + 128, :]), writes=[b_xg[ti]], dma=True)
                def evB(dst, bdst):
                    def ev(ti, rows, half, bk):
                        P.add("act", lambda h: h.activation(out=dst[0:rows, ti, half * 512:(half + 1) * 512], in_=pb[bk][0:rows, :], func=AF.Sigmoid),
                              reads=[b_pb[bk]], writes=[bdst[ti]])
                    return ev
                batches = []
                if (not special) and gi == 3:
                    NBANK[0] = 5
                    batches = sample_batches()

                def cb():
                    if batches:
                        batches.pop(0)()
                win_block(2560, 1024, evB(sga_g, b_sga), after=cb)
                ua = [wload(s_a[c * 128:(c + 1) * 128, :], 1024, cb_a) for c in range(4)]
                for ti, (rows, col0) in enumerate(tiles):
                    for half in range(2):
                        bk = next_bank()

                        def mm_ya(h, rows=rows, col0=col0, half=half, bk=bk):
                            for c in range(4):
                                i = h.matmul(pb[bk][0:rows, :], lhsT=paT[:, c, col0:col0 + rows], rhs=ua[c][0][:, half * 512:(half + 1) * 512], start=(c == 0), stop=(c == 3))
                            return i
                        P.add("pe", mm_ya, reads=[b_paT] + [u[1] for u in ua], writes=[b_pb[bk]])
                        P.add("dve", lambda h, rows=rows, ti=ti, half=half, bk=bk: h.tensor_tensor(out=sga_g[0:rows, ti, half * 512:(half + 1) * 512], in0=pb[bk][0:rows, :],
                                                                                                 in1=sga_g[0:rows, ti, half * 512:(half + 1) * 512], op=ALU.mult),
                              reads=[b_pb[bk], b_sga[ti]], writes=[b_sga[ti]])
                        cb()
                if (not special) and gi == 3:
                    while batches:
                        cb()
                    sample_finish()
                    NBANK[0] = 6
                ugb = [wload(s_in[k * 128:(k + 1) * 128, 3584:4608], 1024, cb_in2[k]) for k in range(8)]
                ub = [wload(s_b[c * 128:(c + 1) * 128, :], 1024, cb_b) for c in range(4)]
                uo = [wload(s_out[j * 128:(j + 1) * 128, :], 1024, cb_out[j // 4]) for j in range(8)]
                pend2 = []

                def stage_M(ti, rows, col0):
                    m_t, b_m = m_tb[ti % 2], b_mb[ti % 2]
                    t1, b_t1 = (g_t, b_g) if ti % 2 == 0 else (T1[:].rearrange("p a b -> p (a b)"), b_T1)
                    for half in range(2):
                        bka, bkb = next_bank(), next_bank()

                        def mm_y(h, half=half, bka=bka, bkb=bkb):
                            for k in range(8):
                                h.matmul(pb[bka][0:rows, :], lhsT=hT[:, k, col0:col0 + rows], rhs=ugb[k][0][:, half * 512:(half + 1) * 512], start=(k == 0), stop=(k == 7))
                            for c in range(4):
                                i = h.matmul(pb[bkb][0:rows, :], lhsT=obT[:, c, col0:col0 + rows], rhs=ub[c][0][:, half * 512:(half + 1) * 512], start=(c == 0), stop=(c == 3))
                            return i
                        P.add("pe", mm_y, reads=[b_hT, b_obT] + [u[1] for u in ugb + ub], writes=[b_pb[bka], b_pb[bkb]])
                        P.add("act", lambda h, bka=bka: h.activation(out=t2[0:rows, :], in_=pb[bka][0:rows, :], func=AF.Sigmoid), reads=[b_pb[bka]], writes=[b_t2])
                        P.add("dve", lambda h, bkb=bkb: h.tensor_tensor(out=t1[0:rows, :], in0=pb[bkb][0:rows, :], in1=t2[0:rows, :], op=ALU.mult),
                              reads=[b_pb[bkb], b_t2], writes=[b_t1])
                        P.add("dve", lambda h, half=half: h.tensor_tensor(out=m_t[0:rows, half * 512:(half + 1) * 512], in0=t1[0:rows, :],
                                                                        in1=sga_g[0:rows, ti, half * 512:(half + 1) * 512], op=ALU.add),
                              reads=[b_t1, b_sga[ti]], writes=[b_m])

                def stage_T(ti, rows, col0):
                    m_t, b_m = m_tb[ti % 2], b_mb[ti % 2]

                    def tr_m(h):
                        for j in range(8):
                            i = h.transpose(out=psT[:, j, 0:rows], in_=m_t[0:rows, j * 128:(j + 1) * 128], identity=csb["ident_b"][0:rows, 0:rows])
                        return i
                    P.add("pe", tr_m, reads=[b_m, cbuf["ident_b"]], writes=[b_pb[7]])
                    P.add("act", lambda h: h.activation(out=mT[:, :, col0:col0 + rows], in_=psT[:, :, 0:rows], func=AF.Copy), reads=[b_pb[7]], writes=[b_mT])
                    if gi == CASTG and PACE:
                        cast_dn(2 * ti, [b_mT])
                        cast_dn(2 * ti + 1, [])

                def stage_O(ti, rows, col0):
                    for half in range(2):
                        bk = next_bank()

                        def mm_o(h, half=half, bk=bk):
                            for j in range(8):
                                i = h.matmul(pb[bk][0:rows, :], lhsT=mT[:, j, col0:col0 + rows], rhs=uo[j][0][:, half * 512:(half + 1) * 512], start=(j == 0), stop=(j == 7))
                            return i
                        P.add("pe", mm_o, reads=[b_mT] + [u[1] for u in uo], writes=[b_pb[bk]])
                        P.add("dve", lambda h, half=half, bk=bk: h.tensor_tensor(out=xg[0:rows, ti, half * 512:(half + 1) * 512], in0=pb[bk][0:rows, :],
                                                                               in1=xg[0:rows, ti, half * 512:(half + 1) * 512], op=ALU.add),
                              reads=[b_pb[bk], b_xg[ti]], writes=[b_xg[ti]])
                    if pend2:
                        pend2.pop(0)()
                    rms_to_T(lambda: xg[0:rows, ti, :], b_xg[ti], rows, gmlp_sb, b_gmlp, col0, hT, b_hT, defer=pend2)

                for step in range(nt + 2):
                    for stg, lag in ((stage_M, 0), (stage_T, 1), (stage_O, 2)):
                        ti = step - lag
                        if 0 <= ti < nt:
                            stg(ti, tiles[ti][0], tiles[ti][1])
                while pend2:
                    pend2.pop(0)()
                ins = {"pre": [], "pe": []}
                if nxt is not None:
                    do_group(nxt, tiles, False, "A1", defer=ins)
                for Fb in range(4):
                    if (not special) and gi < CASTG:
                        uu = [wload_cast(w_up[k * 128:(k + 1) * 128, Fb * 1024:(Fb + 1) * 1024], 1024) for k in range(8)]
                    else:
                        uu = [wload(s_up[k * 128:(k + 1) * 128, Fb * 1024:(Fb + 1) * 1024], 1024, cb_up[k // 2]) for k in range(8)]
                    for fc in range(8):
                        bk = next_bank()

                        def mm_up(h, fc=fc, bk=bk, uu=uu):
                            for k in range(8):
                                i = h.matmul(pb[bk][:, 0:T], lhsT=uu[k][0][:, fc * 128:(fc + 1) * 128], rhs=hT[:, k, 0:T], start=(k == 0), stop=(k == 7))
                            return i
                        P.add("pe", mm_up, reads=[b_hT] + [u[1] for u in uu], writes=[b_pb[bk]])
                        rt, b_rt = (rtmp, b_rtmp) if fc % 2 == 0 else (k_t, b_k)
                        P.add("act", lambda h, bk=bk, rt=rt: h.activation(out=rt[:, 0:T], in_=pb[bk][:, 0:T], func=AF.Relu), reads=[b_pb[bk]], writes=[b_rt])
                        P.add("dve", lambda h, f=Fb * 8 + fc, rt=rt: h.tensor_tensor(out=aT[:, f, 0:T], in0=rt[:, 0:T], in1=rt[:, 0:T], op=ALU.mult), reads=[b_rt], writes=[b_aT, b_aTf[Fb * 8 + fc]])
                        if fc == 1 and ins["pre"]:
                            ins["pre"].pop(0)()
                        if fc == 6 and ins["pe"]:
                            ins["pe"].pop(0)()
                while ins["pre"]:
                    ins["pre"].pop(0)()
                while ins["pe"]:
                    ins["pe"].pop(0)()
                for fc in range(32):
                    if (not special) and gi < CASTG:
                        ud = wload_cast(w_down[fc * 128:(fc + 1) * 128, :], 1024)
                    else:
                        ud = wload(s_down[fc * 128:(fc + 1) * 128, :], 1024, cb_dn[fc // 4])

                    def mm_dn(h, fc=fc, ud=ud):
                        for ti, (rows, col0) in enumerate(tiles):
                            for half in range(2):
                                i = h.matmul(pb[ti * 2 + half][0:rows, :], lhsT=aT[:, fc, col0:col0 + rows], rhs=ud[0][:, half * 512:(half + 1) * 512],
                                             start=(fc == 0), stop=(fc == 31))
                        return i
                    P.add("pe", mm_dn, reads=[b_aTf[fc], ud[1]], writes=[b_pb[i] for i in range(2 * nt)])
                for ti, (rows, col0) in enumerate(tiles):
                    for half in range(2):
                        P.add("dve", lambda h, rows=rows, ti=ti, half=half: h.tensor_tensor(out=xg[0:rows, ti, half * 512:(half + 1) * 512], in0=pb[ti * 2 + half][0:rows, :],
                                                                                          in1=xg[0:rows, ti, half * 512:(half + 1) * 512], op=ALU.add),
                              reads=[b_pb[ti * 2 + half], b_xg[ti]], writes=[b_xg[ti]])
                for ti, (rows, col0) in enumerate(tiles):
                    P.add("act", lambda h, rows=rows, ti=ti: h.activation(out=junk[0:rows, :], in_=xg[0:rows, ti, :], func=AF.Square, scale=1.0 / 32.0, accum_out=st[0:rows, 0:1]),
                          reads=[b_xg[ti]], writes=[b_junk, b_st])
                    P.add("act", lambda h, rows=rows: h.activation(out=st[0:rows, 1:2], in_=st[0:rows, 0:1], func=AF.Ln, bias=eps_sb[0:rows, :], scale=1.0), reads=[b_st, b_eps], writes=[b_st])
                    P.add("act", lambda h, rows=rows: h.activation(out=st[0:rows, 2:3], in_=st[0:rows, 1:2], func=AF.Exp, scale=-0.5), reads=[b_st], writes=[b_st])
                    P.add("dve", lambda h, rows=rows, ti=ti: h.scalar_tensor_tensor(out=xg[0:rows, ti, :], in0=xg[0:rows, ti, :], scalar=st[0:rows, 2:3], in1=gfin_sb[0:rows, :],
                                                                                   op0=ALU.mult, op1=ALU.mult),
                          reads=[b_xg[ti], b_st, b_gfin], writes=[b_xg[ti]])
                    if special:
                        P.add("pool", lambda h: h.dma_start(out=y_s, in_=xg[32:48, 0, :]), reads=[b_xg[0]], dma=True, is_output=True)
                    else:
                        r0 = (gi * 4 + ti) * 128
                        P.add("pool", lambda h, ti=ti, r0=r0: h.dma_start(out=y_p[r0:r0 + 128, :], in_=xg[:, ti, :]), reads=[b_xg[ti]], dma=True, is_output=True)

            if part == "A1":
                part_A1()
                return
            if part == "A":
                part_A1()
                part_A()
            if part == "A2":
                part_A()
            if part == "B":
                part_B()

        def sample_save():
            p0 = pb[0][:, 0:192].rearrange("p (a b) -> p a b", a=12)

            def tr_s(h):
                for j, src in enumerate((sq_gs, fg_gs)):
                    for hh in range(4):
                        h.transpose(out=p0[:, j * 4 + hh, :], in_=src[32:48, 0, hh * 128:(hh + 1) * 128], identity=csb["ident_f"][32:48, 32:48])
                for hh in range(4):
                    i = h.transpose(out=p0[:, 8 + hh, :], in_=k_t[32:48, hh * 128:(hh + 1) * 128], identity=csb["ident_f"][32:48, 32:48])
                return i
            P.add("pe", tr_s, reads=[b_sq_s[0], b_fg_s[0], b_k, cbuf["ident_f"]], writes=[b_pb[0]])
            P.add("dve", lambda h: h.tensor_copy(out=sfT[:], in_=p0), reads=[b_pb[0]], writes=[b_sfT])
            P.add("dve", lambda h: h.tensor_copy(out=sqTb[:], in_=sfT[:, 0:4, :]), reads=[b_sfT], writes=[b_sqTb])
            P.add("dve", lambda h: h.tensor_copy(out=vs_s[32:48, :], in_=v_gs[32:48, 0, :]), reads=[b_v_s[0]], writes=[b_vs])
            P.add("dve", lambda h: h.tensor_copy(out=sog_s[32:48, :], in_=sog_gs[32:48, 0, :]), reads=[b_sog_s[0]], writes=[b_sogs])
            P.add("dve", lambda h: h.memset(obT_s[:], 0.0), writes=[b_obT_s])

        p5 = pb[5][:, 0:64].rearrange("p (a b) -> p a b", a=4)

        def sample_os(k):
            par, bb = k % 2, 2 * k

            def mm_os(h):
                for j in range(2):
                    for hh in range(4):
                        i = h.matmul(p5[:, hh, bb + j:bb + j + 1], lhsT=snbfb2[:, par, j, hh, :], rhs=sqTb[:, hh, bb + j:bb + j + 1], start=True, stop=True)
                return i
            P.add("pe", mm_os, reads=[b_snbf[par], b_sqTb, b_aT], writes=[b_pb[5]])

        def sample_batches():
            P.add("sp", lambda h: h.dma_start(out=csb["eb"], in_=cd["eb"]), reads=[b_aT], writes=[cbuf["eb"]], dma=True)
            out = []
            for k in range(8):
                def batch(k=k):
                    par, bb = k % 2, 2 * k
                    P.add("sp", lambda h: h.dma_start(out=s0b2[:, par], in_=sth[bb:bb + 2].rearrange("b a k v -> k b a v")), reads=[b_aT], writes=[b_s0[par]], dma=True)

                    def mm_vb(h):
                        for j in range(2):
                            i = h.matmul(pb[6 + j][:, :], lhsT=csb["eb"][32:48, bb + j, :], rhs=vs_s[32:48, :], start=True, stop=True)
                        return i
                    P.add("pe", mm_vb, reads=[cbuf["eb"], b_vs, b_aT], writes=[b_pb[6], b_pb[7]])
                    P.add("dve", lambda h: h.tensor_tensor(out=s0b2[:, par], in0=s0b2[:, par],
                                                           in1=sfT[:, 4:8, bb:bb + 2].rearrange("p a b -> p b a").unsqueeze(3).to_broadcast([128, 2, 4, 128]), op=ALU.mult),
                          reads=[b_s0[par], b_sfT, b_aT], writes=[b_s0[par]])
                    for j in range(2):
                        P.add("dve", lambda h, j=j: h.tensor_tensor(out=snewb2[:, par, j], in0=pb[6 + j][:, :].rearrange("p (a b) -> p a b", a=4),
                                                                  in1=sfT[:, 8:12, bb + j:bb + j + 1].to_broadcast([128, 4, 128]), op=ALU.mult),
                              reads=[b_pb[6 + j], b_sfT, b_aT], writes=[b_sn[par]])
                    P.add("dve", lambda h: h.tensor_tensor(out=snewb2[:, par], in0=snewb2[:, par], in1=s0b2[:, par], op=ALU.add), reads=[b_sn[par], b_s0[par], b_aT], writes=[b_sn[par]])
                    P.add("act", lambda h: h.activation(out=snbfb2[:, par], in_=snewb2[:, par], func=AF.Copy), reads=[b_sn[par], b_aT], writes=[b_snbf[par]])
                    P.add("pool", lambda h: h.dma_start(out=nhs[bb:bb + 2].rearrange("b a k v -> k b a v"), in_=snewb2[:, par]), reads=[b_sn[par], b_aT], dma=True, is_output=True)
                    if k > 0:
                        sample_os(k - 1)
                out.append(batch)
            return out

        def sample_finish():
            sample_os(7)
            P.add("act", lambda h: h.activation(out=osT[:], in_=p5, func=AF.Copy), reads=[b_pb[5]], writes=[b_osT])

            def tr_os(h):
                for hh in range(4):
                    i = h.matmul(pb[6][32:48, hh * 128:(hh + 1) * 128], lhsT=osT[:, hh, :], rhs=csb["ident_f"][:, :], start=True, stop=True)
                return i
            P.add("pe", tr_os, reads=[b_osT, cbuf["ident_f"]], writes=[b_pb[6]])
            for hh in range(4):
                P.add("act", lambda h, hh=hh: h.activation(out=junk[32:48, 0:128], in_=pb[6][32:48, hh * 128:(hh + 1) * 128], func=AF.Square,
                                                          scale=float(1.0 / np.sqrt(128.0)), accum_out=st[32:48, 4 + hh:5 + hh]),
                      reads=[b_pb[6]], writes=[b_junk, b_st])
            P.add("act", lambda h: h.activation(out=st[32:48, 8:12], in_=st[32:48, 4:8], func=AF.Ln, bias=eps_sb[32:48, :], scale=1.0), reads=[b_st, b_eps], writes=[b_st])
            P.add("act", lambda h: h.activation(out=st[32:48, 12:16], in_=st[32:48, 8:12], func=AF.Exp, scale=-0.5), reads=[b_st], writes=[b_st])
            P.add("dve", lambda h: h.tensor_tensor(out=ob1[32:48, :].rearrange("p (a b) -> p a b", a=4), in0=pb[6][32:48, :].rearrange("p (a b) -> p a b", a=4),
                                                   in1=st[32:48, 12:16].unsqueeze(2).to_broadcast([16, 4, 128]), op=ALU.mult),
                  reads=[b_pb[6], b_st, b_aT], writes=[b_ob1])
            P.add("dve", lambda h: h.tensor_tensor(out=ob[32:48, :], in0=ob1[32:48, :], in1=sog_s[32:48, :], op=ALU.mult), reads=[b_ob1, b_sogs, b_aT], writes=[b_ob])

            def tr_ob(h):
                for hh in range(4):
                    i = h.transpose(out=psT[:, hh, 0:16], in_=ob[32:48, hh * 128:(hh + 1) * 128], identity=csb["ident_b"][32:48, 32:48])
                return i
            P.add("pe", tr_ob, reads=[b_ob, cbuf["ident_b"]], writes=[b_pb[7]])
            P.add("act", lambda h: h.activation(out=obT_s[:, :, 32:48], in_=psT[:, 0:4, 0:16], func=AF.Copy), reads=[b_pb[7]], writes=[b_obT_s])

        nt4 = [(128, i * 128) for i in range(4)]
        gs_sink = []
        do_group(-1, [(48, 0)], True, "A1")
        do_group(0, nt4, False, "A1")
        do_group(-1, [(48, 0)], True, "A2", sink=gs_sink)
        for gi in range(4):
            do_group(gi, nt4, False, "A2", pre_stages=(gs_sink if gi == 0 else None))
            do_group(gi, nt4, False, "B", nxt=(gi + 1 if gi < 3 else None))
        do_group(-1, [(48, 0)], True, "B")

        P.emit_all(nc, es)
    return nc


_CACHE = {}


def kernel(x_prompt, x_sample, state_pool, state_hgrn, meta_tokens, g_mix, w_in, w_pool,
           pool_scale, hgrn_lb_logits, g_onorm, w_a, w_b, w_out, g_mlp, w_up, w_down, g_final):
    f = lambda a: np.ascontiguousarray(np.asarray(a, dtype=np.float32))
    if "nc" not in _CACHE:
        _CACHE["nc"] = build_program()
        _CACHE["consts"] = _consts()
    nc = _CACHE["nc"]
    consts = _CACHE["consts"]
    x_prompt, x_sample, state_pool, state_hgrn = f(x_prompt), f(x_sample), f(state_pool), f(state_hgrn)
    shared = {
        "meta": f(meta_tokens),
        "w_in": f(w_in)[0], "w_pool": f(np.transpose(f(w_pool)[0], (1, 0, 2))),
        "w_a": f(w_a)[0], "w_b": f(w_b)[0], "w_out": f(w_out)[0], "w_up": f(w_up)[0], "w_down": f(w_down)[0],
        "g_mixT": f(f(g_mix)[0].reshape(8, 128).T), "g_mlpT": f(f(g_mlp)[0].reshape(8, 128).T),
        "pscT": f(f(pool_scale)[0].reshape(4, 128).T),
        "lbl": f(hgrn_lb_logits), "g_on": f(g_onorm).reshape(1, 128), "g_fin": f(g_final).reshape(1, 1024),
    }
    for n, _, _ in _CONST_SPECS:
        shared["c_" + n] = consts[n]
    in_maps = []
    for c in range(NCORES):
        m = dict(shared)
        m["xp"] = x_prompt[c]
        m["xs"] = f(x_sample[16 * c:16 * c + 16, 0, :])
        m["stp"] = f(state_pool[0, 16 * c:16 * c + 16].reshape(240, 512))
        m["sth"] = f(state_hgrn[0, 16 * c:16 * c + 16])
        in_maps.append(m)
    res = run_bass_kernel_spmd(nc, in_maps, core_ids=list(range(NCORES)))
    R = res.results
    _CACHE["last"] = R
    y_prompt = np.stack([np.asarray(R[c]["y_p"]) for c in range(NCORES)], 0).astype(np.float32)
    y_sample = np.concatenate([np.asarray(R[c]["y_s"]) for c in range(NCORES)], 0).reshape(128, 1, 1024).astype(np.float32)
    new_pool_prompt = np.stack([np.asarray(R[c]["npp"]) for c in range(NCORES)], 0)[None].astype(np.float32)
    new_hgrn_prompt = np.stack([np.asarray(R[c]["nhp"]) for c in range(NCORES)], 0)[None].astype(np.float32)
    new_pool_sample = np.concatenate([np.asarray(R[c]["nps"]) for c in range(NCORES)], 0)[None].astype(np.float32)
    new_hgrn_sample = np.concatenate([np.asarray(R[c]["nhs"]) for c in range(NCORES)], 0)[None].astype(np.float32)
    return (y_prompt, y_sample, new_pool_prompt, new_hgrn_prompt, new_pool_sample, new_hgrn_sample)
```

```python
from contextlib import ExitStack
import numpy as np
import ml_dtypes
import concourse.bass as bass
import concourse.mybir as mybir
from concourse.bass_utils import run_bass_kernel_spmd

F32 = mybir.dt.float32
BF16 = mybir.dt.bfloat16
AF = mybir.ActivationFunctionType
ALU = mybir.AluOpType

ENGS = ("pe", "act", "dve", "pool", "sp")
EPS = 1e-6
PACE = True
SKEW = 6
CASTG = 1
NCORES = 8


class Buf:
    __slots__ = ("name", "w", "r")

    def __init__(self, name):
        self.name = name
        self.w = None
        self.r = []


class Op:
    __slots__ = ("eng", "emit", "deps", "sig", "dma", "dsem", "dval", "sigcnt", "prev_same_sem")

    def __init__(self, eng, emit, dma):
        self.eng = eng
        self.emit = emit
        self.dma = dma
        self.deps = []
        self.sig = False
        self.dsem = None
        self.dval = 0
        self.sigcnt = 0
        self.prev_same_sem = None


class Prog:
    def __init__(self, n_dma_sems=24, same_engine_sync=True):
        self.ops = {e: [] for e in ENGS}
        self.n_dma_sems = n_dma_sems
        self.same_engine_sync = same_engine_sync
        self.dma_count = {e: 0 for e in ENGS}
        self.dma_last = {}
        self.out_dmas = []

    def add(self, eng, emit, reads=(), writes=(), dma=False, is_output=False):
        op = Op(eng, emit, dma)
        deps = []
        seen = set()

        def push(d):
            if d is None or d is op or id(d) in seen:
                return
            seen.add(id(d))
            deps.append(d)

        for b in reads:
            push(b.w)
        for b in writes:
            push(b.w)
            for r in b.r:
                push(r)
        fdeps = []
        for d in deps:
            if (not d.dma) and (not dma) and d.eng == eng:
                if eng == "pe" or not self.same_engine_sync:
                    continue
            fdeps.append(d)
        op.deps = fdeps
        for d in fdeps:
            if not d.dma:
                d.sig = True
        for b in reads:
            b.r.append(op)
        for b in writes:
            b.w = op
            b.r = []
        if dma:
            slot = self.dma_count[eng] % self.n_dma_sems
            k = self.dma_count[eng] // self.n_dma_sems
            self.dma_count[eng] += 1
            op.dsem = (eng, slot)
            op.dval = 16 * (k + 1)
            op.prev_same_sem = self.dma_last.get((eng, slot))
            self.dma_last[(eng, slot)] = op
            if is_output:
                self.out_dmas.append(op)
        self.ops[eng].append(op)
        return op

    def emit_all(self, nc, es):
        esem = {e: es.enter_context(nc.semaphore(f"e_{e}")) for e in ENGS}
        dsem = {}
        for e in ENGS:
            n = min(self.dma_count[e], self.n_dma_sems)
            for s in range(n):
                dsem[(e, s)] = es.enter_context(nc.semaphore(f"d_{e}_{s}"))
        for e in ENGS:
            c = 0
            for op in self.ops[e]:
                if op.sig and not op.dma:
                    c += 1
                    op.sigcnt = c
        block = es.enter_context(nc.Block())
        out_dmas = self.out_dmas

        def run_engine(e, h):
            waited = {}

            def wait(semkey, sem, val):
                if waited.get(semkey, 0) >= val:
                    return
                h.wait_ge(sem, val)
                waited[semkey] = val

            for op in self.ops[e]:
                for d in op.deps:
                    if d.dma:
                        wait(("d",) + d.dsem, dsem[d.dsem], d.dval)
                    else:
                        wait(("e", d.eng), esem[d.eng], d.sigcnt)
                if op.dma and op.prev_same_sem is not None:
                    p = op.prev_same_sem
                    wait(("d",) + p.dsem, dsem[p.dsem], p.dval)
                inst = op.emit(h)
                if op.dma:
                    inst.then_inc(dsem[op.dsem], 16)
                elif op.sig:
                    inst.then_inc(esem[e], 1)
            if e == "sp":
                for d in out_dmas:
                    wait(("d",) + d.dsem, dsem[d.dsem], d.dval)

        @block.tensor
        def _(h):
            run_engine("pe", h)

        @block.scalar
        def _(h):
            run_engine("act", h)

        @block.vector
        def _(h):
            run_engine("dve", h)

        @block.gpsimd
        def _(h):
            run_engine("pool", h)

        @block.sync
        def _(h):
            run_engine("sp", h)


def _consts():
    c = {}
    s = np.arange(128)[:, None]
    t = np.arange(128)[None, :]
    c["ident_f"] = np.eye(128, dtype=np.float32)
    c["ident_b"] = np.eye(128).astype(ml_dtypes.bfloat16)
    tri2 = ((s <= t) & (s // 64 == t // 64)).astype(np.float32)
    triS = np.where(t < 16, (s <= t), (s == t)).astype(np.float32)
    maskS = ((t < 16) & (s <= t)).astype(np.float32)
    c["tri"] = np.stack([tri2, triS, tri2, maskS], 1).astype(np.float32)
    ci = np.zeros((128, 4), np.float32)
    ci[:64, 0] = 1
    ci[64:, 1] = 1
    ci[:16, 2] = 1
    c["chunkind"] = ci
    pm = np.zeros((128, 4, 128), np.float32)
    pp = np.zeros((128, 4, 128), np.float32)
    ppm = np.zeros((16, 4, 128), np.float32)
    psp = np.zeros((48, 4, 48), np.float32)
    psel = np.zeros((120, 4, 8), np.float32)
    for g, w in enumerate((2, 4, 8, 16)):
        for tt in range(128):
            for ss in range(tt - w + 1, tt + 1):
                if ss >= 0:
                    pm[ss, g, tt] += 1.0 / w
                else:
                    pp[128 + ss, g, tt] += 1.0 / w
                    if 16 + ss >= 0:
                        ppm[16 + ss, g, tt] += 1.0 / w
            pm[tt, g, tt] -= 1.0
        for tt in range(16):
            cnt = min(tt + 1, w)
            for ss in range(max(0, tt - w + 1), tt + 1):
                psp[ss, g, tt] += 1.0 / cnt
            psp[tt, g, tt] -= 1.0
        for b in range(16):
            psp[32 + b, g, 32 + b] = 1.0 / w - 1.0
        for b in range(8):
            for r in range(15):
                if r - 15 > -w:
                    psel[b * 15 + r, g, b] = 1.0 / w
    c["pmain"] = pm
    c["pprev"] = pp
    c["pprevmeta"] = ppm
    c["pspec"] = psp
    c["psel"] = psel
    eb = np.zeros((48, 16, 128), np.float32)
    for b in range(16):
        eb[32 + b, b, :] = 1.0
    c["eb"] = eb.astype(ml_dtypes.bfloat16)
    return c


_CONST_SPECS = [
    ("ident_f", [128, 128], F32), ("ident_b", [128, 128], BF16), ("tri", [128, 4, 128], F32),
    ("chunkind", [128, 4], F32), ("pmain", [128, 4, 128], F32), ("pprev", [128, 4, 128], F32),
    ("pprevmeta", [16, 4, 128], F32), ("pspec", [48, 4, 48], F32), ("psel", [120, 4, 8], F32),
    ("eb", [48, 16, 128], BF16),
]


def build_program(debug=False):
    nc = bass.Bass("TRN2", target_bir_lowering=False, dynamic_dma_scratch_size=4096)
    P = Prog()
    dbg_outs = {}

    def din(name, shape, dt=F32):
        return nc.dram_tensor(name, shape, dt, kind="ExternalInput").ap()

    def dout(name, shape, dt=F32):
        return nc.dram_tensor(name, shape, dt, kind="ExternalOutput").ap()

    xp = din("xp", [2048, 1024])
    xs = din("xs", [16, 1024])
    stp = din("stp", [240, 512])
    sth = din("sth", [16, 4, 128, 128])
    meta = din("meta", [16, 1024])
    w_in = din("w_in", [1024, 4608])
    w_pool = din("w_pool", [128, 4, 128])
    w_a = din("w_a", [512, 1024])
    w_b = din("w_b", [512, 1024])
    w_out = din("w_out", [1024, 1024])
    w_up = din("w_up", [1024, 4096])
    w_down = din("w_down", [4096, 1024])
    g_mixT = din("g_mixT", [128, 8])
    g_mlpT = din("g_mlpT", [128, 8])
    pscT = din("pscT", [128, 4])
    lbl = din("lbl", [2, 512])
    g_on = din("g_on", [1, 128])
    g_fin = din("g_fin", [1, 1024])
    cd = {n: din("c_" + n, sh, dt) for n, sh, dt in _CONST_SPECS}

    y_p = dout("y_p", [2048, 1024])
    y_s = dout("y_s", [16, 1024])
    npp = dout("npp", [15, 512])
    nhp = dout("nhp", [4, 128, 128])
    nps = dout("nps", [16, 15, 512])
    nhs = dout("nhs", [16, 4, 128, 128])

    def scratch(name, shape):
        return nc.dram_tensor(name, shape, BF16, kind="Internal").ap()

    s_in = scratch("s_in", [1024, 4608])
    s_a = scratch("s_a", [512, 1024])
    s_b = scratch("s_b", [512, 1024])
    s_out = scratch("s_out", [1024, 1024])
    s_up = scratch("s_up", [1024, 4096])
    s_down = scratch("s_down", [4096, 1024])

    with ExitStack() as es:
        def sb(name, shape, dt=F32):
            return es.enter_context(nc.sbuf_tensor(name, shape, dt))

        def emit_casts(first_deps):
            for k in range(8):
                P.add("pool", lambda h, k=k: h.dma_start(out=s_in[k * 128:(k + 1) * 128, 0:2560], in_=w_in[k * 128:(k + 1) * 128, 0:2560], max_dma_last_dim=4096),
                      reads=(first_deps if k == 0 else []), writes=[cb_in[k]], dma=True)

        def emit_casts2(deps):
            for k in range(8):
                P.add("pool", lambda h, k=k: h.dma_start(out=s_in[k * 128:(k + 1) * 128, 2560:4608], in_=w_in[k * 128:(k + 1) * 128, 2560:4608], max_dma_last_dim=4096),
                      reads=(deps if k == 0 else []), writes=[cb_in2[k]], dma=True)
            P.add("pool", lambda h: h.dma_start(out=s_a, in_=w_a, max_dma_last_dim=4096), writes=[cb_a], dma=True)
            P.add("pool", lambda h: h.dma_start(out=s_b, in_=w_b, max_dma_last_dim=4096), writes=[cb_b], dma=True)
            for k in range(2):
                P.add("pool", lambda h, k=k: h.dma_start(out=s_out[k * 512:(k + 1) * 512, :], in_=w_out[k * 512:(k + 1) * 512, :], max_dma_last_dim=4096), writes=[cb_out[k]], dma=True)

        def cast_up(k, deps):
            P.add("pool", lambda h: h.dma_start(out=s_up[k * 256:(k + 1) * 256, :], in_=w_up[k * 256:(k + 1) * 256, :], max_dma_last_dim=4096), reads=deps, writes=[cb_up[k]], dma=True)

        def cast_dn(k, deps):
            P.add("pool", lambda h: h.dma_start(out=s_down[k * 512:(k + 1) * 512, :], in_=w_down[k * 512:(k + 1) * 512, :], max_dma_last_dim=4096), reads=deps, writes=[cb_dn[k]], dma=True)
        cb_in = [Buf(f"cin{k}") for k in range(8)]
        cb_in2 = [Buf(f"cin2{k}") for k in range(8)]
        cb_a, cb_b = Buf("ca"), Buf("cb")
        cb_out = [Buf(f"cout{k}") for k in range(2)]
        cb_up = [Buf(f"cup{k}") for k in range(4)]
        cb_dn = [Buf(f"cdn{k}") for k in range(8)]

        aT = sb("aT", [128, 32, 512], BF16)
        b_aT = Buf("aT")
        b_aTf = [Buf(f"aTf{i}") for i in range(32)]
        csb, cbuf = {}, {}
        for n, sh, dt in _CONST_SPECS:
            cbuf[n] = Buf("k_" + n)
            if n == "eb":
                csb[n] = aT[0:48, 14:18, :].rearrange("p a (b c) -> p (a b) c", c=128)
                P.add("sp", lambda h, n=n: h.dma_start(out=csb[n], in_=cd[n]), reads=[b_aT], writes=[cbuf[n]], dma=True)
            else:
                csb[n] = sb("k_" + n, sh, dt)
                P.add("sp", lambda h, n=n: h.dma_start(out=csb[n][:], in_=cd[n]), writes=[cbuf[n]], dma=True)

        def small(name, shape, src, dt=F32):
            t = sb(name, shape, dt)
            b = Buf(name)
            P.add("sp", lambda h: h.dma_start(out=t[:], in_=src), writes=[b], dma=True)
            return t, b

        gmix_sb, b_gmix = small("gmix", [128, 8], g_mixT)
        gmlp_sb, b_gmlp = small("gmlp", [128, 8], g_mlpT)
        psc_sb, b_psc = small("psc", [128, 4], pscT)
        gfin_sb, b_gfin = small("gfin", [128, 1024], g_fin.to_broadcast([128, 1024]))
        l0_sb, b_l0 = small("l0", [128, 512], lbl[0:1, :].to_broadcast([128, 512]))
        l1_sb, b_l1 = small("l1", [128, 512], lbl[1:2, :].to_broadcast([128, 512]))
        gon_sb = sb("gon", [128, 4, 128])
        b_gon = Buf("gon")
        for hh in range(4):
            P.add("sp", lambda h, hh=hh: h.dma_start(out=gon_sb[:, hh, :], in_=g_on.to_broadcast([128, 128])), writes=[b_gon], dma=True)
        rtmp = sb("rtmp", [128, 512]); b_rtmp = Buf("rtmp")
        wpool_f = rtmp[:].rearrange("p (a b) -> p a b", a=4)
        b_wpf = b_rtmp
        P.add("sp", lambda h: h.dma_start(out=wpool_f, in_=w_pool), writes=[b_wpf], dma=True)
        wpool_sb = sb("wpool", [128, 4, 128], BF16)
        b_wp = Buf("wp")
        P.add("dve", lambda h: h.tensor_copy(out=wpool_sb[:], in_=wpool_f), reads=[b_wpf], writes=[b_wp])
        stp_sb = aT[0:120, 10:14, :].rearrange("p a b -> p (a b)").bitcast(F32).rearrange("p (j c) -> p j c", j=2)
        b_stp = Buf("stp")
        for j in range(2):
            P.add("sp", lambda h, j=j: h.dma_start(out=stp_sb[:, j, :], in_=stp[j * 120:(j + 1) * 120, :]), reads=[b_aT], writes=[b_stp], dma=True)

        eps_sb = sb("eps", [128, 1])
        b_eps = Buf("eps")
        P.add("dve", lambda h: h.memset(eps_sb[:], EPS), writes=[b_eps])
        lb_sb = l0_sb
        oml_sb = l1_sb
        b_lb, b_oml = b_l0, b_l1
        P.add("dve", lambda h: h.tensor_tensor(out=lb_sb[:], in0=l0_sb[:], in1=l1_sb[:], op=ALU.subtract), reads=[b_l0, b_l1], writes=[b_lb])
        P.add("act", lambda h: h.activation(out=lb_sb[:], in_=lb_sb[:], func=AF.Sigmoid), reads=[b_lb], writes=[b_lb])
        P.add("dve", lambda h: h.tensor_scalar(out=oml_sb[:], in0=lb_sb[:], scalar1=-1.0, scalar2=1.0, op0=ALU.mult, op1=ALU.add), reads=[b_lb], writes=[b_oml])

        if not PACE:
            for k in range(4):
                cast_up(k, [])
            for k in range(8):
                cast_dn(k, [])
        first_cast_deps = [[cbuf[n] for n, _, _ in _CONST_SPECS] + [b_gmix, b_gmlp, b_psc, b_gfin, b_l0, b_l1, b_gon, b_wpf, b_stp]]
        NS = 20
        ring = sb("ring", [128, NS, 1024], BF16)
        ring_b = [Buf(f"ring{i}") for i in range(NS)]
        ring_pos = [0]

        def wload_cast(src32, width, deps=()):
            i = ring_pos[0] % NS
            ring_pos[0] += 1
            P.add("pool", lambda h: h.dma_start(out=ring[:, i, 0:width], in_=src32, max_dma_last_dim=4096), reads=list(deps), writes=[ring_b[i]], dma=True)
            return ring[:, i, :], ring_b[i]

        def wload(src, width, dep):
            i = ring_pos[0] % NS
            ring_pos[0] += 1
            P.add("sp", lambda h: h.dma_start(out=ring[:, i, 0:width], in_=src), reads=[dep], writes=[ring_b[i]], dma=True)
            return ring[:, i, :], ring_b[i]

        xg_n = sb("xg", [128, 4, 1024])
        b_xg_n = [Buf(f"xg{i}") for i in range(4)]
        hT_nb = [sb("hT", [128, 8, 512], BF16), sb("hT2", [128, 8, 512], BF16)]
        b_hT_nb = [[Buf(f"hT_{i}") for i in range(4)], [Buf(f"hT2_{i}") for i in range(4)]]
        xstage = sb("xstage", [128, 1024])
        b_xstage = Buf("xstage")
        xg_s = sb("xg_s", [128, 1, 1024])
        b_xg_s = [Buf("xg_s")]
        hT_s = sb("hT_s", [128, 8, 48], BF16)
        b_hT_s = [Buf("hT_s")]
        obT_s = sb("obT_s", [128, 4, 48], BF16)
        b_obT_s = Buf("obT_s")
        paT_s = sb("paT_s", [128, 4, 48], BF16)
        b_paT_s = Buf("paT_s")
        xn = sb("xn", [128, 1024], BF16)
        b_xn = Buf("xn")
        junk = rtmp[:].bitcast(BF16)
        b_junk = b_rtmp
        st = sb("st", [128, 16])
        b_st = Buf("st")
        u_g = sb("u_g", [128, 4, 512])
        b_u = [Buf(f"u{i}") for i in range(4)]
        u_prev = sb("u_prev", [128, 512])
        b_uprev = Buf("uprev")
        sq_g = sb("sq_g", [128, 4, 512])
        b_sq = [Buf(f"sq{i}") for i in range(4)]
        fg_g = sb("fg_g", [128, 4, 512])
        b_fg = [Buf(f"fg{i}") for i in range(4)]
        v_g = sb("v_g", [128, 4, 512], BF16)
        b_v = [Buf(f"v{i}") for i in range(4)]
        sog_g = sb("sog_g", [128, 4, 512], BF16)
        b_sog = [Buf(f"sog{i}") for i in range(4)]
        sga_g = sb("sga_g", [128, 4, 1024], BF16)
        b_sga = [Buf(f"sga{i}") for i in range(4)]
        obT_n = sb("obT", [128, 4, 512], BF16)
        b_obT_n = Buf("obT")
        paT_n = sb("paT", [128, 4, 512], BF16)
        b_paT_n = Buf("paT")
        mT = sb("mT", [128, 8, 512], BF16)
        b_mT = [Buf(f"mT{i}") for i in range(4)]
        g_t = sb("g_t", [128, 512]); b_g = Buf("g_t")
        k_t = sb("k_t", [128, 512]); b_k = Buf("k_t")
        Ep = sb("Ep", [128, 512]); b_Ep = Buf("Ep")
        En = sb("En", [128, 512]); b_En = Buf("En")
        dG2 = sb("dG2", [128, 2, 8]); b_dG2 = [Buf("dG2a"), Buf("dG2b")]
        qt = sb("qt", [128, 512], BF16); b_qt = Buf("qt")
        kt = sb("kt", [128, 512], BF16); b_kt = Buf("kt")
        qkT = sb("qkT", [128, 8, 128], BF16); b_qkT = Buf("qkT")
        AT = sb("AT", [128, 4, 128], BF16); b_AT = Buf("AT")
        S = sb("S", [128, 4, 128]); b_S = Buf("S")
        S_bf = sb("S_bf", [128, 4, 128], BF16); b_Sbf = Buf("S_bf")
        T1 = sb("T1", [128, 4, 128]); b_T1 = Buf("T1")
        ob1 = aT[:, 30:32, :].rearrange("p a b -> p (a b)").bitcast(F32); b_ob1 = Buf("ob1")
        ob = sb("ob", [128, 512], BF16); b_ob = Buf("ob")
        pooled = sb("pooled", [128, 4, 128], BF16); b_pooled = Buf("pooled")
        t1 = g_t; b_t1 = b_g
        t2 = k_t; b_t2 = b_k
        m_t = En[:].bitcast(BF16); b_m = b_En
        m_tb = [En[:].bitcast(BF16), Ep[:].bitcast(BF16)]; b_mb = [b_En, b_Ep]
        sfT = sb("sfT", [128, 12, 16]); b_sfT = Buf("sfT")
        sqTb = sb("sqTb", [128, 4, 16], BF16); b_sqTb = Buf("sqTb")
        s0buf = aT[:, 0:4, :].rearrange("p a b -> p (a b)").bitcast(F32).rearrange("p (s a v) -> p s a v", s=2, a=4); b_s0 = [Buf("s0a"), Buf("s0b")]
        snew = aT[:, 4:8, :].rearrange("p a b -> p (a b)").bitcast(F32).rearrange("p (s a v) -> p s a v", s=2, a=4); b_sn = [Buf("sna"), Buf("snb")]
        snbf = aT[:, 8:10, :].rearrange("p a b -> p (a b)").rearrange("p (s a v) -> p s a v", s=2, a=4); b_snbf = [Buf("snbfa"), Buf("snbfb")]
        osT = sb("osT", [128, 4, 16]); b_osT = Buf("osT")
        vs_s = sb("vs_s", [128, 512], BF16); b_vs = Buf("vs_s")
        sog_s = sb("sog_s", [128, 512], BF16); b_sogs = Buf("sog_s")
        s0b2 = aT[:, 0:8, :].rearrange("p a b -> p (a b)").bitcast(F32).rearrange("p (q s a v) -> p q s a v", q=2, s=2, a=4)
        snewb2 = aT[:, 22:30, :].rearrange("p a b -> p (a b)").bitcast(F32).rearrange("p (q s a v) -> p q s a v", q=2, s=2, a=4)
        snbfb2 = aT[:, 18:22, :].rearrange("p a b -> p (a b)").rearrange("p (q s a v) -> p q s a v", q=2, s=2, a=4)
        NBANK = [6]
        s0b = aT[:, 0:8, :].rearrange("p a b -> p (a b)").bitcast(F32).rearrange("p (s a v) -> p s a v", s=4, a=4); b_s0b = Buf("s0b")
        snewb = aT[:, 22:30, :].rearrange("p a b -> p (a b)").bitcast(F32).rearrange("p (s a v) -> p s a v", s=4, a=4); b_snewb = Buf("snewb")
        snbfb = aT[:, 18:22, :].rearrange("p a b -> p (a b)").rearrange("p (s a v) -> p s a v", s=4, a=4); b_snbfb = Buf("snbfb")

        pb = [es.enter_context(nc.psum_tensor(f"pb{i}", [128, 512], F32)) for i in range(8)]
        b_pb = [Buf(f"pb{i}") for i in range(8)]
        psT = pb[7][:].bitcast(BF16).rearrange("p (k n) -> p k n", k=8)

        P.add("dve", lambda h: h.memset(S[:], 0.0), writes=[b_S])
        P.add("dve", lambda h: h.memset(S_bf[:], 0.0), writes=[b_Sbf])

        def dump(name, ap, bufs, rows, cols):
            if not debug:
                return
            d = nc.dram_tensor("dbg_" + name, [rows, cols], F32, kind="ExternalOutput").ap()
            if "stg" not in dbg_outs:
                dbg_outs["stg"] = (sb("dstg", [128, 512]), Buf("dstg"))
            stg, bst = dbg_outs["stg"]
            P.add("dve", lambda h: h.tensor_copy(out=stg[0:rows, :], in_=ap), reads=bufs, writes=[bst])
            P.add("sp", lambda h: h.dma_start(out=d, in_=stg[0:rows, :]), reads=[bst], dma=True, is_output=True)

        def rms_to_T(src_ap_fn, b_src, rows, gT_sb, b_gT, col0, hT, b_hT, defer=None):
            P.add("act", lambda h: h.activation(out=junk[0:rows, :], in_=src_ap_fn(), func=AF.Square, scale=1.0 / 32.0, accum_out=st[0:rows, 0:1]),
                  reads=[b_src], writes=[b_junk, b_st])
            P.add("act", lambda h: h.activation(out=st[0:rows, 1:2], in_=st[0:rows, 0:1], func=AF.Ln, bias=eps_sb[0:rows, :], scale=1.0),
                  reads=[b_st, b_eps], writes=[b_st])
            P.add("act", lambda h: h.activation(out=st[0:rows, 2:3], in_=st[0:rows, 1:2], func=AF.Exp, scale=-0.5), reads=[b_st], writes=[b_st])
            P.add("act", lambda h: h.activation(out=xn[0:rows, :], in_=src_ap_fn(), func=AF.Copy, scale=st[0:rows, 2:3]), reads=[b_src, b_st], writes=[b_xn])

            def pe_part():
                def tr(h):
                    for k in range(8):
                        i = h.transpose(out=psT[:, k, 0:rows], in_=xn[0:rows, k * 128:(k + 1) * 128], identity=csb["ident_b"][0:rows, 0:rows])
                    return i
                P.add("pe", tr, reads=[b_xn, cbuf["ident_b"]], writes=[b_pb[7]])
                P.add("dve", lambda h: h.tensor_tensor(out=hT[:, :, col0:col0 + rows], in0=psT[:, :, 0:rows],
                                                       in1=gT_sb[:].unsqueeze(2).to_broadcast([128, 8, rows]), op=ALU.mult),
                      reads=[b_pb[7], b_gT], writes=[b_hT])
            if defer is None:
                pe_part()
            else:
                defer.append(pe_part)

        bank_rr = [0]

        def next_bank(nb=None):
            nb = nb or NBANK[0]
            i = bank_rr[0] % nb
            bank_rr[0] += 1
            return i

        def do_group(gi, tiles, special, part="AB", defer=None, nxt=None):
            nt = len(tiles)
            T = tiles[-1][1] + tiles[-1][0]
            if special:
                xg, b_xg, hT, b_hT, obT, b_obT, paT, b_paT = xg_s, b_xg_s, hT_s, b_hT_s, obT_s, b_obT_s, paT_s, b_paT_s
            else:
                xg, b_xg, hT, b_hT, obT, b_obT, paT, b_paT = xg_n, b_xg_n, hT_nb[gi % 2], b_hT_nb[gi % 2], obT_n, b_obT_n, paT_n, b_paT_n
            def win_block(c0, width, evac, after=None):
                if c0 < 2560 and (gi <= 0) and part != "B":
                    units = []
                    for k in range(8):
                        d = first_cast_deps.pop() if first_cast_deps else ()
                        units.append(wload_cast(w_in[k * 128:(k + 1) * 128, c0:c0 + width], width, d))
                else:
                    units = [wload(s_in[k * 128:(k + 1) * 128, c0:c0 + width], width, (cb_in if c0 < 2560 else cb_in2)[k]) for k in range(8)]
                for ti, (rows, col0) in enumerate(tiles):
                    for half in range(width // 512):
                        bk = next_bank()

                        def mm(h, rows=rows, col0=col0, half=half, bk=bk):
                            for k in range(8):
                                i = h.matmul(pb[bk][0:rows, :], lhsT=hT[:, k, col0:col0 + rows], rhs=units[k][0][:, half * 512:(half + 1) * 512],
                                             start=(k == 0), stop=(k == 7))
                            return i
                        P.add("pe", mm, reads=[b_hT[ti]] + [u[1] for u in units], writes=[b_pb[bk]])
                        evac(ti, rows, half, bk)
                        if after is not None:
                            after()
                return units

            def part_A1():
                if special:
                    P.add("dve", lambda h: h.memset(xg[0:48, 0, :], 0.0), writes=[b_xg[0]])
                    P.add("sp", lambda h: h.dma_start(out=xg[0:16, 0, :], in_=meta), writes=[b_xg[0]], dma=True)
                    P.add("sp", lambda h: h.dma_start(out=xg[32:48, 0, :], in_=xs), writes=[b_xg[0]], dma=True)
                    rms_to_T(lambda: xg[0:48, 0, :], b_xg[0], 48, gmix_sb, b_gmix, 0, hT, b_hT[0])
                else:
                    for ti, (rows, col0) in enumerate(tiles):
                        def pre(ti=ti, rows=rows, col0=col0):
                            r0 = (gi * 4 + ti) * 128
                            P.add("sp", lambda h: h.dma_start(out=xstage[:, :], in_=xp[r0:r0 + 128, :]), writes=[b_xstage], dma=True)
                            rms_to_T(lambda: xstage[:, :], b_xstage, rows, gmix_sb, b_gmix, col0, hT, b_hT[ti], defer=(None if defer is None else defer["pe"]))
                        if defer is None:
                            pre()
                        else:
                            defer["pre"].append(pre)

            def part_A():
                def evA0(ti, rows, half, bk):
                    if half == 0:
                        P.add("act", lambda h: h.activation(out=u_g[0:rows, ti, :], in_=pb[bk][0:rows, :], func=AF.Copy), reads=[b_pb[bk]], writes=[b_u[ti]])
                    else:
                        P.add("act", lambda h: h.activation(out=sq_g[0:rows, ti, :], in_=pb[bk][0:rows, :], func=AF.Silu), reads=[b_pb[bk]], writes=[b_sq[ti]])

                def evA1(ti, rows, half, bk):
                    if half == 0:
                        P.add("act", lambda h: h.activation(out=fg_g[0:rows, ti, :], in_=pb[bk][0:rows, :], func=AF.Sigmoid), reads=[b_pb[bk]], writes=[b_fg[ti]])
                        P.add("dve", lambda h: h.tensor_tensor(out=fg_g[0:rows, ti, :], in0=fg_g[0:rows, ti, :], in1=oml_sb[0:rows, :], op=ALU.mult),
                              reads=[b_fg[ti], b_oml], writes=[b_fg[ti]])
                        P.add("dve", lambda h: h.tensor_tensor(out=fg_g[0:rows, ti, :], in0=fg_g[0:rows, ti, :], in1=lb_sb[0:rows, :], op=ALU.add),
                              reads=[b_fg[ti], b_lb], writes=[b_fg[ti]])
                    else:
                        P.add("act", lambda h: h.activation(out=v_g[0:rows, ti, :], in_=pb[bk][0:rows, :], func=AF.Copy), reads=[b_pb[bk]], writes=[b_v[ti]])

                def evA2(ti, rows, half, bk):
                    P.add("act", lambda h: h.activation(out=sog_g[0:rows, ti, :], in_=pb[bk][0:rows, :], func=AF.Silu), reads=[b_pb[bk]], writes=[b_sog[ti]])
                    P.add("dve", lambda h: h.tensor_tensor(out=sog_g[0:rows, ti, :], in0=sog_g[0:rows, ti, :],
                                                           in1=gon_sb[0:rows, :, :].rearrange("p a b -> p (a b)"), op=ALU.mult),
                          reads=[b_sog[ti], b_gon], writes=[b_sog[ti]])

                win_block(0, 1024, evA0)
                win_block(1024, 1024, evA1)
                uB2 = win_block(2048, 512, evA2)
                if gi == 0 and not special:
                    emit_casts2([uB2[-1][1]])
                    emit_casts([])

                tsel = 1 if special else 0
                chunks = [(0, 16)] if special else [(0, 64), (64, 64)]
                ccol = 2 if special else 0
                msel = 3 if special else 2
                pa2 = pb[2][:].rearrange("p (a b) -> p a b", a=4)
                p6 = pb[6][:].rearrange("p (a b) -> p a b", a=4)
                p1 = pb[1][:].rearrange("p (a b) -> p a b", a=4)
                psT6 = pb[6][:].bitcast(BF16).rearrange("p (k n) -> p k n", k=8)

                def tile_stages(ti, r, col0):
                    par = ti % 2
                    dGp = dG2[:, par, :]
                    b_dGp = b_dG2[par]
                    st_ = []

                    def s0():
                        P.add("act", lambda h: h.activation(out=g_t[0:r, :], in_=fg_g[0:r, ti, :], func=AF.Ln), reads=[b_fg[ti]], writes=[b_g])
                        P.add("dve", lambda h: h.tensor_scalar(out=k_t[0:r, :], in0=fg_g[0:r, ti, :], scalar1=-1.0, scalar2=1.0, op0=ALU.mult, op1=ALU.add),
                              reads=[b_fg[ti]], writes=[b_k])
                    st_.append(s0)

                    def s1():
                        P.add("pe", lambda h: h.matmul(pb[0][0:r, :], lhsT=csb["tri"][0:r, tsel, 0:r], rhs=g_t[0:r, :], start=True, stop=True),
                              reads=[b_g, cbuf["tri"]], writes=[b_pb[0]])

                        def mm_dg(h):
                            for hh in range(4):
                                i = h.matmul(pb[1][:, hh * 2:hh * 2 + 2], lhsT=g_t[0:r, hh * 128:(hh + 1) * 128], rhs=csb["chunkind"][0:r, ccol:ccol + 2], start=True, stop=True)
                            return i
                        P.add("pe", mm_dg, reads=[b_g, cbuf["chunkind"]], writes=[b_pb[1]])
                    st_.append(s1)

                    def s2():
                        P.add("act", lambda h: h.activation(out=Ep[0:r, :], in_=pb[0][0:r, :], func=AF.Exp), reads=[b_pb[0]], writes=[b_Ep])
                        P.add("act", lambda h: h.activation(out=En[0:r, :], in_=pb[0][0:r, :], func=AF.Exp, scale=-1.0), reads=[b_pb[0]], writes=[b_En])
                        P.add("act", lambda h: h.activation(out=dGp, in_=pb[1][:, 0:8], func=AF.Exp), reads=[b_pb[1]], writes=[b_dGp])
                    st_.append(s2)

                    def s3():
                        P.add("dve", lambda h: h.tensor_tensor(out=qt[0:r, :], in0=sq_g[0:r, ti, :], in1=Ep[0:r, :], op=ALU.mult), reads=[b_sq[ti], b_Ep], writes=[b_qt])
                        P.add("dve", lambda h: h.tensor_tensor(out=kt[0:r, :], in0=k_t[0:r, :], in1=En[0:r, :], op=ALU.mult), reads=[b_k, b_En], writes=[b_kt])
                    st_.append(s3)

                    def s4():
                        def tr_qk(h):
                            for hh in range(4):
                                h.transpose(out=psT[:, hh, 0:r], in_=qt[0:r, hh * 128:(hh + 1) * 128], identity=csb["ident_b"][0:r, 0:r])
                            for hh in range(4):
                                i = h.transpose(out=psT[:, 4 + hh, 0:r], in_=kt[0:r, hh * 128:(hh + 1) * 128], identity=csb["ident_b"][0:r, 0:r])
                            return i
                        P.add("pe", tr_qk, reads=[b_qt, b_kt, cbuf["ident_b"]], writes=[b_pb[7]])
                    st_.append(s4)

                    def s5():
                        P.add("dve", lambda h: h.tensor_copy(out=qkT[:, :, 0:r], in_=psT[:, :, 0:r]), reads=[b_pb[7]], writes=[b_qkT])
                    st_.append(s5)

                    def s6():
                        def mm_A(h):
                            for hh in range(4):
                                i = h.matmul(pa2[0:r, hh, 0:r], lhsT=qkT[:, 4 + hh, 0:r], rhs=qkT[:, hh, 0:r], start=True, stop=True)
                            return i
                        P.add("pe", mm_A, reads=[b_qkT], writes=[b_pb[2]])

                        def mm_U(h):
                            for ci, (rs, cs) in enumerate(chunks):
                                for hh in range(4):
                                    i = h.matmul(pb[4 + ci][:, hh * 128:(hh + 1) * 128], lhsT=kt[rs:rs + cs, hh * 128:(hh + 1) * 128],
                                                 rhs=v_g[rs:rs + cs, ti, hh * 128:(hh + 1) * 128], start=True, stop=True)
                            return i
                        P.add("pe", mm_U, reads=[b_kt, b_v[ti]], writes=[b_pb[4 + ci] for ci in range(len(chunks))])
                    st_.append(s6)

                    def s7():
                        P.add("dve", lambda h: h.tensor_tensor(out=AT[0:r, :, 0:r], in0=pa2[0:r, :, 0:r],
                                                               in1=csb["tri"][0:r, msel:msel + 1, 0:r].to_broadcast([r, 4, r]), op=ALU.mult),
                              reads=[b_pb[2], cbuf["tri"]], writes=[b_AT])
                    st_.append(s7)

                    def mk_ox(ci, rs, cs, with_intra):
                        def sx():
                            if with_intra:
                                def mm_oi(h):
                                    for hh in range(4):
                                        i = h.matmul(pb[3][0:r, hh * 128:(hh + 1) * 128], lhsT=AT[0:r, hh, 0:r], rhs=v_g[0:r, ti, hh * 128:(hh + 1) * 128], start=(hh == 0), stop=False)
                                    return i
                                P.add("pe", mm_oi, reads=[b_AT, b_v[ti]], writes=[b_pb[3]])

                            def mm_ox(h):
                                for hh in range(4):
                                    i = h.matmul(pb[3][rs:rs + cs, hh * 128:(hh + 1) * 128], lhsT=qkT[:, hh, rs:rs + cs], rhs=S_bf[:, hh, :], start=False,
                                                 stop=(hh == 3))
                                return i
                            P.add("pe", mm_ox, reads=[b_qkT, b_Sbf], writes=[b_pb[3]])
                        return sx

                    def mk_upd(ci):
                        def su():
                            dgb = dGp.rearrange("p (a c) -> p a c", c=2)[:, :, ci:ci + 1].to_broadcast([128, 4, 128])
                            P.add("dve", lambda h: h.tensor_tensor(out=T1[:], in0=pb[4 + ci][:, :].rearrange("p (a b) -> p a b", a=4), in1=S[:], op=ALU.add),
                                  reads=[b_pb[4 + ci], b_S], writes=[b_T1])
                            P.add("dve", lambda h: h.tensor_tensor(out=S[:], in0=T1[:], in1=dgb, op=ALU.mult), reads=[b_T1, b_dGp], writes=[b_S])
                            P.add("act", lambda h: h.activation(out=S_bf[:], in_=S[:], func=AF.Copy), reads=[b_S], writes=[b_Sbf])
                        return su
                    if special:
                        st_.append(mk_upd(0))
                        st_.append(sample_save)
                    else:
                        for ci, (rs, cs) in enumerate(chunks):
                            st_.append(mk_ox(ci, rs, cs, ci == 0))
                            st_.append(mk_upd(ci))

                    def s12():
                        for hh in range(4):
                            P.add("act", lambda h, hh=hh: h.activation(out=junk[0:r, 0:128], in_=pb[3][0:r, hh * 128:(hh + 1) * 128], func=AF.Square,
                                                                      scale=float(1.0 / np.sqrt(128.0)), accum_out=st[0:r, 4 + hh:5 + hh]),
                                  reads=[b_pb[3]], writes=[b_junk, b_st])
                        P.add("act", lambda h: h.activation(out=st[0:r, 8:12], in_=st[0:r, 4:8], func=AF.Ln, bias=eps_sb[0:r, :], scale=1.0), reads=[b_st, b_eps], writes=[b_st])
                        P.add("act", lambda h: h.activation(out=st[0:r, 12:16], in_=st[0:r, 8:12], func=AF.Exp, scale=-0.5), reads=[b_st], writes=[b_st])
                        P.add("dve", lambda h: h.tensor_tensor(out=ob1[0:r, :].rearrange("p (a b) -> p a b", a=4), in0=pb[3][0:r, :].rearrange("p (a b) -> p a b", a=4),
                                                               in1=st[0:r, 12:16].unsqueeze(2).to_broadcast([r, 4, 128]), op=ALU.mult),
                              reads=[b_pb[3], b_st, b_aT], writes=[b_ob1])
                        P.add("dve", lambda h: h.tensor_tensor(out=ob[0:r, :], in0=ob1[0:r, :], in1=sog_g[0:r, ti, :], op=ALU.mult), reads=[b_ob1, b_sog[ti], b_aT], writes=[b_ob])
                    if not special:
                        st_.append(s12)

                    def s13():
                        def tr_ob(h):
                            for hh in range(4):
                                i = h.transpose(out=psT6[:, hh, 0:r], in_=ob[0:r, hh * 128:(hh + 1) * 128], identity=csb["ident_b"][0:r, 0:r])
                            return i
                        P.add("pe", tr_ob, reads=[b_ob, cbuf["ident_b"]], writes=[b_pb[6]])
                    if not special:
                        st_.append(s13)

                    def s14():
                        P.add("act", lambda h: h.activation(out=obT[:, :, col0:col0 + r], in_=psT6[:, 0:4, 0:r], func=AF.Copy), reads=[b_pb[6]], writes=[b_obT])
                    if not special:
                        st_.append(s14)

                    def s15():
                        def mm_pool(h):
                            for gg in range(4):
                                if special:
                                    h.matmul(p6[:, gg, 0:48], lhsT=u_g[0:48, 0, gg * 128:(gg + 1) * 128], rhs=csb["pspec"][0:48, gg, :], start=True, stop=False)
                                    for j in range(2):
                                        i = h.matmul(p6[:, gg, 32 + 8 * j:40 + 8 * j], lhsT=stp_sb[:, j, gg * 128:(gg + 1) * 128], rhs=csb["psel"][:, gg, :], start=False, stop=(j == 1))
                                else:
                                    h.matmul(p6[:, gg, :], lhsT=u_g[:, ti, gg * 128:(gg + 1) * 128], rhs=csb["pmain"][:, gg, :], start=True, stop=False)
                                    if ti > 0:
                                        i = h.matmul(p6[:, gg, :], lhsT=u_g[:, ti - 1, gg * 128:(gg + 1) * 128], rhs=csb["pprev"][:, gg, :], start=False, stop=True)
                                    elif gi == 0:
                                        i = h.matmul(p6[:, gg, :], lhsT=u_prev[0:16, gg * 128:(gg + 1) * 128], rhs=csb["pprevmeta"][:, gg, :], start=False, stop=True)
                                    else:
                                        i = h.matmul(p6[:, gg, :], lhsT=u_prev[:, gg * 128:(gg + 1) * 128], rhs=csb["pprev"][:, gg, :], start=False, stop=True)
                            return i
                        rd = [b_u[ti], cbuf["pmain"], cbuf["pprev"], cbuf["pprevmeta"], cbuf["pspec"], cbuf["psel"]]
                        if special:
                            rd += [b_stp, b_aT]
                        rd.append(b_u[ti - 1] if ti > 0 else b_uprev)
                        P.add("pe", mm_pool, reads=rd, writes=[b_pb[6]])
                    st_.append(s15)

                    def s16():
                        P.add("act", lambda h: h.activation(out=pooled[:, :, 0:r], in_=p6[:, :, 0:r], func=AF.Copy), reads=[b_pb[6]], writes=[b_pooled])
                    st_.append(s16)

                    def s17():
                        def mm_wp(h):
                            for gg in range(4):
                                i = h.matmul(p1[:, gg, 0:r], lhsT=wpool_sb[:, gg, :], rhs=pooled[:, gg, 0:r], start=True, stop=True)
                            return i
                        P.add("pe", mm_wp, reads=[b_pooled, b_wp], writes=[b_pb[1]])
                    st_.append(s17)

                    def s18():
                        P.add("dve", lambda h: h.tensor_tensor(out=paT[:, :, col0:col0 + r], in0=p1[:, :, 0:r],
                                                               in1=psc_sb[:].unsqueeze(2).to_broadcast([128, 4, r]), op=ALU.mult),
                              reads=[b_pb[1], b_psc], writes=[b_paT])
                        if gi == CASTG and PACE:
                            cast_up(ti, [b_paT])
                    st_.append(s18)
                    return st_

                all_st = [tile_stages(ti, rows, col0) for ti, (rows, col0) in enumerate(tiles)]
                nst = len(all_st[0])
                SK = SKEW
                for step in range(nst + SK * (nt - 1)):
                    for ti in range(nt):
                        k = step - SK * ti
                        if 0 <= k < nst:
                            all_st[ti][k]()


                lt = nt - 1
                if special:
                    P.add("dve", lambda h: h.tensor_copy(out=u_prev[0:16, :], in_=u_g[0:16, 0, :]), reads=[b_u[0]], writes=[b_uprev])
                    P.add("pool", lambda h: h.dma_start(out=nps[:, 0:14, :], in_=stp.rearrange("(b r) c -> b r c", r=15)[:, 1:15, :]), dma=True, is_output=True)
                    P.add("pool", lambda h: h.dma_start(out=nps[:, 14, :], in_=u_g[32:48, 0, :]), reads=[b_u[0]], dma=True, is_output=True)
                else:
                    P.add("dve", lambda h: h.tensor_copy(out=u_prev[:, :], in_=u_g[:, lt, :]), reads=[b_u[lt]], writes=[b_uprev])
                    if gi == 3:
                        P.add("pool", lambda h: h.dma_start(out=npp, in_=u_g[113:128, 3, :]), reads=[b_u[3]], dma=True, is_output=True)
                        P.add("pool", lambda h: h.dma_start(out=nhp.rearrange("a k v -> k a v"), in_=S[:]), reads=[b_S], dma=True, is_output=True)

            def part_B():
                if not special:
                    for ti in range(nt):
                        r0 = (gi * 4 + ti) * 128
                        P.add("sp", lambda h, ti=ti, r0=r0: h.dma_start(out=xg[:, ti, :], in_=xp[r0:r0 + 128, :]), writes=[b_xg[ti]], dma=True)
                def evB(dst, bdst):
                    def ev(ti, rows, half, bk):
                        P.add("act", lambda h: h.activation(out=dst[0:rows, ti, half * 512:(half + 1) * 512], in_=pb[bk][0:rows, :], func=AF.Sigmoid),
                              reads=[b_pb[bk]], writes=[bdst[ti]])
                    return ev
                batches = []
                if (not special) and gi == 3:
                    NBANK[0] = 5
                    batches = sample_batches()

                def cb():
                    if batches:
                        batches.pop(0)()
                win_block(2560, 1024, evB(sga_g, b_sga), after=cb)
                ua = [wload(s_a[c * 128:(c + 1) * 128, :], 1024, cb_a) for c in range(4)]
                for ti, (rows, col0) in enumerate(tiles):
                    for half in range(2):
                        bk = next_bank()

                        def mm_ya(h, rows=rows, col0=col0, half=half, bk=bk):
                            for c in range(4):
                                i = h.matmul(pb[bk][0:rows, :], lhsT=paT[:, c, col0:col0 + rows], rhs=ua[c][0][:, half * 512:(half + 1) * 512], start=(c == 0), stop=(c == 3))
                            return i
                        P.add("pe", mm_ya, reads=[b_paT] + [u[1] for u in ua], writes=[b_pb[bk]])
                        P.add("dve", lambda h, rows=rows, ti=ti, half=half, bk=bk: h.tensor_tensor(out=sga_g[0:rows, ti, half * 512:(half + 1) * 512], in0=pb[bk][0:rows, :],
                                                                                                 in1=sga_g[0:rows, ti, half * 512:(half + 1) * 512], op=ALU.mult),
                              reads=[b_pb[bk], b_sga[ti]], writes=[b_sga[ti]])
                        cb()
                if (not special) and gi == 3:
                    while batches:
                        cb()
                    sample_finish()
                    NBANK[0] = 6
                ugb = [wload(s_in[k * 128:(k + 1) * 128, 3584:4608], 1024, cb_in2[k]) for k in range(8)]
                ub = [wload(s_b[c * 128:(c + 1) * 128, :], 1024, cb_b) for c in range(4)]
                uo = [wload(s_out[j * 128:(j + 1) * 128, :], 1024, cb_out[j // 4]) for j in range(8)]
                pend2 = []

                def stage_M(ti, rows, col0):
                    m_t, b_m = m_tb[ti % 2], b_mb[ti % 2]
                    t1, b_t1 = (g_t, b_g) if ti % 2 == 0 else (T1[:].rearrange("p a b -> p (a b)"), b_T1)
                    for half in range(2):
                        bka, bkb = next_bank(), next_bank()

                        def mm_y(h, half=half, bka=bka, bkb=bkb):
                            for k in range(8):
                                h.matmul(pb[bka][0:rows, :], lhsT=hT[:, k, col0:col0 + rows], rhs=ugb[k][0][:, half * 512:(half + 1) * 512], start=(k == 0), stop=(k == 7))
                            for c in range(4):
                                i = h.matmul(pb[bkb][0:rows, :], lhsT=obT[:, c, col0:col0 + rows], rhs=ub[c][0][:, half * 512:(half + 1) * 512], start=(c == 0), stop=(c == 3))
                            return i
                        P.add("pe", mm_y, reads=[b_hT[ti], b_obT] + [u[1] for u in ugb + ub], writes=[b_pb[bka], b_pb[bkb]])
                        P.add("act", lambda h, bka=bka: h.activation(out=t2[0:rows, :], in_=pb[bka][0:rows, :], func=AF.Sigmoid), reads=[b_pb[bka]], writes=[b_t2])
                        P.add("dve", lambda h, bkb=bkb: h.tensor_tensor(out=t1[0:rows, :], in0=pb[bkb][0:rows, :], in1=t2[0:rows, :], op=ALU.mult),
                              reads=[b_pb[bkb], b_t2], writes=[b_t1])
                        P.add("dve", lambda h, half=half: h.tensor_tensor(out=m_t[0:rows, half * 512:(half + 1) * 512], in0=t1[0:rows, :],
                                                                        in1=sga_g[0:rows, ti, half * 512:(half + 1) * 512], op=ALU.add),
                              reads=[b_t1, b_sga[ti]], writes=[b_m])

                def stage_T(ti, rows, col0):
                    m_t, b_m = m_tb[ti % 2], b_mb[ti % 2]

                    def tr_m(h):
                        for j in range(8):
                            i = h.transpose(out=psT[:, j, 0:rows], in_=m_t[0:rows, j * 128:(j + 1) * 128], identity=csb["ident_b"][0:rows, 0:rows])
                        return i
                    P.add("pe", tr_m, reads=[b_m, cbuf["ident_b"]], writes=[b_pb[7]])
                    P.add("act", lambda h: h.activation(out=mT[:, :, col0:col0 + rows], in_=psT[:, :, 0:rows], func=AF.Copy), reads=[b_pb[7]], writes=[b_mT[ti]])
                    if gi == CASTG and PACE:
                        cast_dn(2 * ti, [b_mT[ti]])
                        cast_dn(2 * ti + 1, [])

                def stage_O(ti, rows, col0):
                    for half in range(2):
                        bk = next_bank()

                        def mm_o(h, half=half, bk=bk):
                            for j in range(8):
                                i = h.matmul(pb[bk][0:rows, :], lhsT=mT[:, j, col0:col0 + rows], rhs=uo[j][0][:, half * 512:(half + 1) * 512], start=(j == 0), stop=(j == 7))
                            return i
                        P.add("pe", mm_o, reads=[b_mT[ti]] + [u[1] for u in uo], writes=[b_pb[bk]])
                        P.add("dve", lambda h, half=half, bk=bk: h.tensor_tensor(out=xg[0:rows, ti, half * 512:(half + 1) * 512], in0=pb[bk][0:rows, :],
                                                                               in1=xg[0:rows, ti, half * 512:(half + 1) * 512], op=ALU.add),
                              reads=[b_pb[bk], b_xg[ti]], writes=[b_xg[ti]])
                    if pend2:
                        pend2.pop(0)()
                    rms_to_T(lambda: xg[0:rows, ti, :], b_xg[ti], rows, gmlp_sb, b_gmlp, col0, hT, b_hT[ti], defer=pend2)

                for step in range(nt + 2):
                    for stg, lag in ((stage_M, 0), (stage_T, 1), (stage_O, 2)):
                        ti = step - lag
                        if 0 <= ti < nt:
                            stg(ti, tiles[ti][0], tiles[ti][1])
                while pend2:
                    pend2.pop(0)()
                ins = {"pre": [], "pe": []}
                if nxt is not None:
                    do_group(nxt, tiles, False, "A1", defer=ins)
                for Fb in range(4):
                    if (not special) and gi < CASTG:
                        uu = [wload_cast(w_up[k * 128:(k + 1) * 128, Fb * 1024:(Fb + 1) * 1024], 1024) for k in range(8)]
                    else:
                        uu = [wload(s_up[k * 128:(k + 1) * 128, Fb * 1024:(Fb + 1) * 1024], 1024, cb_up[k // 2]) for k in range(8)]
                    for fc in range(8):
                        bk = next_bank()

                        def mm_up(h, fc=fc, bk=bk, uu=uu):
                            for k in range(8):
                                i = h.matmul(pb[bk][:, 0:T], lhsT=uu[k][0][:, fc * 128:(fc + 1) * 128], rhs=hT[:, k, 0:T], start=(k == 0), stop=(k == 7))
                            return i
                        P.add("pe", mm_up, reads=list(b_hT) + [u[1] for u in uu], writes=[b_pb[bk]])
                        rt, b_rt = (rtmp, b_rtmp) if fc % 2 == 0 else (k_t, b_k)
                        P.add("act", lambda h, bk=bk, rt=rt: h.activation(out=rt[:, 0:T], in_=pb[bk][:, 0:T], func=AF.Relu), reads=[b_pb[bk]], writes=[b_rt])
                        P.add("dve", lambda h, f=Fb * 8 + fc, rt=rt: h.tensor_tensor(out=aT[:, f, 0:T], in0=rt[:, 0:T], in1=rt[:, 0:T], op=ALU.mult), reads=[b_rt], writes=[b_aT, b_aTf[Fb * 8 + fc]])
                        if fc == 1 and ins["pre"]:
                            ins["pre"].pop(0)()
                        if fc == 6 and ins["pe"]:
                            ins["pe"].pop(0)()
                while ins["pre"]:
                    ins["pre"].pop(0)()
                while ins["pe"]:
                    ins["pe"].pop(0)()
                for fc in range(32):
                    if (not special) and gi < CASTG:
                        ud = wload_cast(w_down[fc * 128:(fc + 1) * 128, :], 1024)
                    else:
                        ud = wload(s_down[fc * 128:(fc + 1) * 128, :], 1024, cb_dn[fc // 4])

                    def mm_dn(h, fc=fc, ud=ud):
                        for ti, (rows, col0) in enumerate(tiles):
                            for half in range(2):
                                i = h.matmul(pb[ti * 2 + half][0:rows, :], lhsT=aT[:, fc, col0:col0 + rows], rhs=ud[0][:, half * 512:(half + 1) * 512],
                                             start=(fc == 0), stop=(fc == 31))
                        return i
                    P.add("pe", mm_dn, reads=[b_aTf[fc], ud[1]], writes=[b_pb[i] for i in range(2 * nt)])
                for ti, (rows, col0) in enumerate(tiles):
                    for half in range(2):
                        P.add("dve", lambda h, rows=rows, ti=ti, half=half: h.tensor_tensor(out=xg[0:rows, ti, half * 512:(half + 1) * 512], in0=pb[ti * 2 + half][0:rows, :],
                                                                                          in1=xg[0:rows, ti, half * 512:(half + 1) * 512], op=ALU.add),
                              reads=[b_pb[ti * 2 + half], b_xg[ti]], writes=[b_xg[ti]])
                for ti, (rows, col0) in enumerate(tiles):
                    P.add("act", lambda h, rows=rows, ti=ti: h.activation(out=junk[0:rows, :], in_=xg[0:rows, ti, :], func=AF.Square, scale=1.0 / 32.0, accum_out=st[0:rows, 0:1]),
                          reads=[b_xg[ti]], writes=[b_junk, b_st])
                    P.add("act", lambda h, rows=rows: h.activation(out=st[0:rows, 1:2], in_=st[0:rows, 0:1], func=AF.Ln, bias=eps_sb[0:rows, :], scale=1.0), reads=[b_st, b_eps], writes=[b_st])
                    P.add("act", lambda h, rows=rows: h.activation(out=st[0:rows, 2:3], in_=st[0:rows, 1:2], func=AF.Exp, scale=-0.5), reads=[b_st], writes=[b_st])
                    P.add("dve", lambda h, rows=rows, ti=ti: h.scalar_tensor_tensor(out=xg[0:rows, ti, :], in0=xg[0:rows, ti, :], scalar=st[0:rows, 2:3], in1=gfin_sb[0:rows, :],
                                                                                   op0=ALU.mult, op1=ALU.mult),
                          reads=[b_xg[ti], b_st, b_gfin], writes=[b_xg[ti]])
                    if special:
                        P.add("pool", lambda h: h.dma_start(out=y_s, in_=xg[32:48, 0, :]), reads=[b_xg[0]], dma=True, is_output=True)
                    else:
                        r0 = (gi * 4 + ti) * 128
                        P.add("pool", lambda h, ti=ti, r0=r0: h.dma_start(out=y_p[r0:r0 + 128, :], in_=xg[:, ti, :]), reads=[b_xg[ti]], dma=True, is_output=True)

            if part == "A1":
                part_A1()
                return
            if part == "A":
                part_A1()
                part_A()
            if part == "A2":
                part_A()
            if part == "B":
                part_B()

        def sample_save():
            p0 = pb[0][:, 0:192].rearrange("p (a b) -> p a b", a=12)

            def tr_s(h):
                for j, src in enumerate((sq_g, fg_g)):
                    for hh in range(4):
                        h.transpose(out=p0[:, j * 4 + hh, :], in_=src[32:48, 0, hh * 128:(hh + 1) * 128], identity=csb["ident_f"][32:48, 32:48])
                for hh in range(4):
                    i = h.transpose(out=p0[:, 8 + hh, :], in_=k_t[32:48, hh * 128:(hh + 1) * 128], identity=csb["ident_f"][32:48, 32:48])
                return i
            P.add("pe", tr_s, reads=[b_sq[0], b_fg[0], b_k, cbuf["ident_f"]], writes=[b_pb[0]])
            P.add("dve", lambda h: h.tensor_copy(out=sfT[:], in_=p0), reads=[b_pb[0]], writes=[b_sfT])
            P.add("dve", lambda h: h.tensor_copy(out=sqTb[:], in_=sfT[:, 0:4, :]), reads=[b_sfT], writes=[b_sqTb])
            P.add("dve", lambda h: h.tensor_copy(out=vs_s[32:48, :], in_=v_g[32:48, 0, :]), reads=[b_v[0]], writes=[b_vs])
            P.add("dve", lambda h: h.tensor_copy(out=sog_s[32:48, :], in_=sog_g[32:48, 0, :]), reads=[b_sog[0]], writes=[b_sogs])
            P.add("dve", lambda h: h.memset(obT_s[:], 0.0), writes=[b_obT_s])

        p5 = pb[5][:, 0:64].rearrange("p (a b) -> p a b", a=4)

        def sample_os(k):
            par, bb = k % 2, 2 * k

            def mm_os(h):
                for j in range(2):
                    for hh in range(4):
                        i = h.matmul(p5[:, hh, bb + j:bb + j + 1], lhsT=snbfb2[:, par, j, hh, :], rhs=sqTb[:, hh, bb + j:bb + j + 1], start=True, stop=True)
                return i
            P.add("pe", mm_os, reads=[b_snbf[par], b_sqTb, b_aT], writes=[b_pb[5]])

        def sample_batches():
            P.add("sp", lambda h: h.dma_start(out=csb["eb"], in_=cd["eb"]), reads=[b_aT], writes=[cbuf["eb"]], dma=True)
            out = []
            for k in range(8):
                def batch(k=k):
                    par, bb = k % 2, 2 * k
                    P.add("sp", lambda h: h.dma_start(out=s0b2[:, par], in_=sth[bb:bb + 2].rearrange("b a k v -> k b a v")), reads=[b_aT], writes=[b_s0[par]], dma=True)

                    def mm_vb(h):
                        for j in range(2):
                            i = h.matmul(pb[6 + j][:, :], lhsT=csb["eb"][32:48, bb + j, :], rhs=vs_s[32:48, :], start=True, stop=True)
                        return i
                    P.add("pe", mm_vb, reads=[cbuf["eb"], b_vs, b_aT], writes=[b_pb[6], b_pb[7]])
                    P.add("dve", lambda h: h.tensor_tensor(out=s0b2[:, par], in0=s0b2[:, par],
                                                           in1=sfT[:, 4:8, bb:bb + 2].rearrange("p a b -> p b a").unsqueeze(3).to_broadcast([128, 2, 4, 128]), op=ALU.mult),
                          reads=[b_s0[par], b_sfT, b_aT], writes=[b_s0[par]])
                    for j in range(2):
                        P.add("dve", lambda h, j=j: h.tensor_tensor(out=snewb2[:, par, j], in0=pb[6 + j][:, :].rearrange("p (a b) -> p a b", a=4),
                                                                  in1=sfT[:, 8:12, bb + j:bb + j + 1].to_broadcast([128, 4, 128]), op=ALU.mult),
                              reads=[b_pb[6 + j], b_sfT, b_aT], writes=[b_sn[par]])
                    P.add("dve", lambda h: h.tensor_tensor(out=snewb2[:, par], in0=snewb2[:, par], in1=s0b2[:, par], op=ALU.add), reads=[b_sn[par], b_s0[par], b_aT], writes=[b_sn[par]])
                    P.add("act", lambda h: h.activation(out=snbfb2[:, par], in_=snewb2[:, par], func=AF.Copy), reads=[b_sn[par], b_aT], writes=[b_snbf[par]])
                    P.add("pool", lambda h: h.dma_start(out=nhs[bb:bb + 2].rearrange("b a k v -> k b a v"), in_=snewb2[:, par]), reads=[b_sn[par], b_aT], dma=True, is_output=True)
                    if k > 0:
                        sample_os(k - 1)
                out.append(batch)
            return out

        def sample_finish():
            sample_os(7)
            P.add("act", lambda h: h.activation(out=osT[:], in_=p5, func=AF.Copy), reads=[b_pb[5]], writes=[b_osT])

            def tr_os(h):
                for hh in range(4):
                    i = h.matmul(pb[6][32:48, hh * 128:(hh + 1) * 128], lhsT=osT[:, hh, :], rhs=csb["ident_f"][:, :], start=True, stop=True)
                return i
            P.add("pe", tr_os, reads=[b_osT, cbuf["ident_f"]], writes=[b_pb[6]])
            for hh in range(4):
                P.add("act", lambda h, hh=hh: h.activation(out=junk[32:48, 0:128], in_=pb[6][32:48, hh * 128:(hh + 1) * 128], func=AF.Square,
                                                          scale=float(1.0 / np.sqrt(128.0)), accum_out=st[32:48, 4 + hh:5 + hh]),
                      reads=[b_pb[6]], writes=[b_junk, b_st])
            P.add("act", lambda h: h.activation(out=st[32:48, 8:12], in_=st[32:48, 4:8], func=AF.Ln, bias=eps_sb[32:48, :], scale=1.0), reads=[b_st, b_eps], writes=[b_st])
            P.add("act", lambda h: h.activation(out=st[32:48, 12:16], in_=st[32:48, 8:12], func=AF.Exp, scale=-0.5), reads=[b_st], writes=[b_st])
            P.add("dve", lambda h: h.tensor_tensor(out=ob1[32:48, :].rearrange("p (a b) -> p a b", a=4), in0=pb[6][32:48, :].rearrange("p (a b) -> p a b", a=4),
                                                   in1=st[32:48, 12:16].unsqueeze(2).to_broadcast([16, 4, 128]), op=ALU.mult),
                  reads=[b_pb[6], b_st, b_aT], writes=[b_ob1])
            P.add("dve", lambda h: h.tensor_tensor(out=ob[32:48, :], in0=ob1[32:48, :], in1=sog_s[32:48, :], op=ALU.mult), reads=[b_ob1, b_sogs, b_aT], writes=[b_ob])

            def tr_ob(h):
                for hh in range(4):
                    i = h.transpose(out=psT[:, hh, 0:16], in_=ob[32:48, hh * 128:(hh + 1) * 128], identity=csb["ident_b"][32:48, 32:48])
                return i
            P.add("pe", tr_ob, reads=[b_ob, cbuf["ident_b"]], writes=[b_pb[7]])
            P.add("act", lambda h: h.activation(out=obT_s[:, :, 32:48], in_=psT[:, 0:4, 0:16], func=AF.Copy), reads=[b_pb[7]], writes=[b_obT_s])

        nt4 = [(128, i * 128) for i in range(4)]
        do_group(-1, [(48, 0)], True, "A1")
        do_group(0, nt4, False, "A1")
        do_group(-1, [(48, 0)], True, "A2")
        for gi in range(4):
            do_group(gi, nt4, False, "A2")
            do_group(gi, nt4, False, "B", nxt=(gi + 1 if gi < 3 else None))
        do_group(-1, [(48, 0)], True, "B")

        P.emit_all(nc, es)
    return nc


_CACHE = {}


def kernel(x_prompt, x_sample, state_pool, state_hgrn, meta_tokens, g_mix, w_in, w_pool,
           pool_scale, hgrn_lb_logits, g_onorm, w_a, w_b, w_out, g_mlp, w_up, w_down, g_final):
    f = lambda a: np.ascontiguousarray(np.asarray(a, dtype=np.float32))
    if "nc" not in _CACHE:
        _CACHE["nc"] = build_program()
        _CACHE["consts"] = _consts()
    nc = _CACHE["nc"]
    consts = _CACHE["consts"]
    x_prompt, x_sample, state_pool, state_hgrn = f(x_prompt), f(x_sample), f(state_pool), f(state_hgrn)
    shared = {
        "meta": f(meta_tokens),
        "w_in": f(w_in)[0], "w_pool": f(np.transpose(f(w_pool)[0], (1, 0, 2))),
        "w_a": f(w_a)[0], "w_b": f(w_b)[0], "w_out": f(w_out)[0], "w_up": f(w_up)[0], "w_down": f(w_down)[0],
        "g_mixT": f(f(g_mix)[0].reshape(8, 128).T), "g_mlpT": f(f(g_mlp)[0].reshape(8, 128).T),
        "pscT": f(f(pool_scale)[0].reshape(4, 128).T),
        "lbl": f(hgrn_lb_logits), "g_on": f(g_onorm).reshape(1, 128), "g_fin": f(g_final).reshape(1, 1024),
    }
    for n, _, _ in _CONST_SPECS:
        shared["c_" + n] = consts[n]
    in_maps = []
    for c in range(NCORES):
        m = dict(shared)
        m["xp"] = x_prompt[c]
        m["xs"] = f(x_sample[16 * c:16 * c + 16, 0, :])
        m["stp"] = f(state_pool[0, 16 * c:16 * c + 16].reshape(240, 512))
        m["sth"] = f(state_hgrn[0, 16 * c:16 * c + 16])
        in_maps.append(m)
    res = run_bass_kernel_spmd(nc, in_maps, core_ids=list(range(NCORES)))
    R = res.results
    _CACHE["last"] = R
    y_prompt = np.stack([np.asarray(R[c]["y_p"]) for c in range(NCORES)], 0).astype(np.float32)
    y_sample = np.concatenate([np.asarray(R[c]["y_s"]) for c in range(NCORES)], 0).reshape(128, 1, 1024).astype(np.float32)
    new_pool_prompt = np.stack([np.asarray(R[c]["npp"]) for c in range(NCORES)], 0)[None].astype(np.float32)
    new_hgrn_prompt = np.stack([np.asarray(R[c]["nhp"]) for c in range(NCORES)], 0)[None].astype(np.float32)
    new_pool_sample = np.concatenate([np.asarray(R[c]["nps"]) for c in range(NCORES)], 0)[None].astype(np.float32)
    new_hgrn_sample = np.concatenate([np.asarray(R[c]["nhs"]) for c in range(NCORES)], 0)[None].astype(np.float32)
    return (y_prompt, y_sample, new_pool_prompt, new_hgrn_prompt, new_pool_sample, new_hgrn_sample)
```
